# Optimizing a Trainium2 kernel written in Bass

```python
import functools
import math
import jax
import jax.numpy as jnp
from jax import lax
import numpy as np

D_MODEL = 1024
BATCH = 4
SEQ = 4096
DEPTH = 2
DEC_BATCH = 32
DEC_SEQ = 64
PAST_LEN = 1024

CHUNK = 64
HEAD_DIM = 64
A_HEADS = 8
A_PAST_CHUNKS = 8
A_REL_CLIP = 128
B_HEADS = 8
C_HEADS = 4
C_VDIM = 2 * HEAD_DIM
T5_BUCKETS = 32
T5_MAX_DIST = 128
D_FF = 2816
CONV_W = 3
N_BRANCH = 3
A_W = A_HEADS * HEAD_DIM
B_W = B_HEADS * HEAD_DIM
C_QK_W = C_HEADS * 2 * HEAD_DIM
C_V_W = C_HEADS * C_VDIM
BRANCH_W = A_W
IN_SPLITS = (A_W, A_W, A_W, B_W, B_W, B_W, C_QK_W, C_QK_W, C_V_W, N_BRANCH * D_MODEL)
IN_COLS = sum(IN_SPLITS)
QBLK = 128
EPS = 1e-6
NEG_INF = -1e30

kernel_name = 'streaming_hybrid_gated_trunk'


def rmsnorm(x, g):
    xf = x.astype(jnp.float32)
    y = xf * lax.rsqrt(jnp.mean(xf * xf, axis=-1, keepdims=True) + EPS)
    return (y * g.astype(jnp.float32)).astype(x.dtype)


def t5_bucket(rel):
    half = T5_BUCKETS // 2
    max_exact = half // 2
    ret = jnp.where(rel > 0, half, 0)
    n = jnp.abs(rel)
    nf = jnp.maximum(n, 1).astype(jnp.float32)
    large = max_exact + (jnp.log(nf / max_exact) / math.log(T5_MAX_DIST / max_exact) * (half - max_exact)).astype(jnp.int32)
    large = jnp.minimum(large, half - 1)
    return ret + jnp.where(n < max_exact, n, large)


def t5_bucket_bias(q_pos, k_pos, table):
    b = table[t5_bucket(k_pos[None, :] - q_pos[:, None])]
    return jnp.moveaxis(b, -1, 0).astype(jnp.float32)


def clipped_rel_bias(rel, table):
    b = table[jnp.clip(rel, -A_REL_CLIP, A_REL_CLIP) + A_REL_CLIP]
    return jnp.moveaxis(b, -1, 0).astype(jnp.float32)


def mixer_inputs(xn, w_in):
    b, t = xn.shape[0], xn.shape[1]
    p = jnp.einsum('btd,dc->btc', xn, w_in)
    offs = np.cumsum((0,) + IN_SPLITS)
    parts = [p[..., int(offs[i]):int(offs[i + 1])] for i in range(len(IN_SPLITS))]
    qa, ka, va, qb, kb, vb, qc, kc, vc, gate_pre = parts
    return (qa.reshape(b, t, A_HEADS, HEAD_DIM), ka.reshape(b, t, A_HEADS, HEAD_DIM),
            va.reshape(b, t, A_HEADS, HEAD_DIM),
            qb.reshape(b, t, B_HEADS, HEAD_DIM), kb.reshape(b, t, B_HEADS, HEAD_DIM),
            vb.reshape(b, t, B_HEADS, HEAD_DIM),
            qc.reshape(b, t, C_HEADS, 2, HEAD_DIM), kc.reshape(b, t, C_HEADS, 2, HEAD_DIM),
            vc.reshape(b, t, C_HEADS, C_VDIM), gate_pre)


def band_attention_prompt(q, k, v, rel_table):
    b, s, h, d = q.shape
    nc = s // CHUNK
    nb = A_PAST_CHUNKS + 1
    qc = q.reshape(b, nc, CHUNK, h, d)
    pad = ((0, 0), (A_PAST_CHUNKS, 0), (0, 0), (0, 0), (0, 0))
    kp = jnp.pad(k.reshape(b, nc, CHUNK, h, d), pad)
    vp = jnp.pad(v.reshape(b, nc, CHUNK, h, d), pad)
    idx = jnp.arange(nc)[:, None] + jnp.arange(nb)[None, :]
    kband = kp[:, idx].reshape(b, nc, nb * CHUNK, h, d)
    vband = vp[:, idx].reshape(b, nc, nb * CHUNK, h, d)
    i = jnp.arange(CHUNK)
    j = jnp.arange(nb)
    k_off = ((j[:, None] - A_PAST_CHUNKS) * CHUNK + i[None, :]).reshape(-1)
    bias = clipped_rel_bias(i[:, None] - k_off[None, :], rel_table)
    valid = jnp.repeat(idx >= A_PAST_CHUNKS, CHUNK, axis=1)
    logits = jnp.einsum('bnqhd,bnkhd->bhnqk', qc, kband).astype(jnp.float32) * (HEAD_DIM ** -0.5)
    logits = logits + bias[None, :, None]
    logits = jnp.where(valid[None, None, :, None, :], logits, NEG_INF)
    p = jax.nn.softmax(logits, axis=-1)
    o = jnp.einsum('bhnqk,bnkhd->bnqhd', p.astype(v.dtype), vband)
    return o.reshape(b, s, h * d)


def band_attention_sample(q, k_new, v_new, k_cache, v_cache, rel_table, past):
    b, t = q.shape[0], q.shape[1]
    keep = k_cache.shape[1]
    keys = jnp.concatenate([k_cache, k_new], axis=1)
    vals = jnp.concatenate([v_cache, v_new], axis=1)
    q_pos = past + jnp.arange(t, dtype=jnp.int32)
    k_pos = past - keep + jnp.arange(keep + t, dtype=jnp.int32)
    bias = clipped_rel_bias(q_pos[:, None] - k_pos[None, :], rel_table)
    logits = jnp.einsum('bqhd,bkhd->bhqk', q, keys).astype(jnp.float32) * (HEAD_DIM ** -0.5) + bias[None]
    p = jax.nn.softmax(logits, axis=-1)
    o = jnp.einsum('bhqk,bkhd->bqhd', p.astype(vals.dtype), vals).reshape(b, t, -1)
    return o, keys[:, t:], vals[:, t:]


def stick_breaking_block(q, k, v, q_pos, k_pos):
    z = jnp.einsum('bqhd,bkhd->bhqk', q, k).astype(jnp.float32) * (HEAD_DIM ** -0.5)
    mask = (k_pos[None, :] < q_pos[:, None])[None, None]
    log_keep = jnp.where(mask, jax.nn.log_sigmoid(-z), 0.0)
    rc = lax.cumsum(log_keep, axis=3, reverse=True)
    after = jnp.concatenate([rc[..., 1:], jnp.zeros_like(rc[..., :1])], axis=-1)
    w = jnp.where(mask, jnp.exp(jax.nn.log_sigmoid(z) + after), 0.0)
    return jnp.einsum('bhqk,bkhd->bqhd', w.astype(v.dtype), v)


def diff_lambda(lam_params, layer_idx):
    lam_init = 0.8 - 0.6 * math.exp(-0.3 * layer_idx)
    lp = lam_params.astype(jnp.float32)
    lam = jnp.exp(jnp.sum(lp[0] * lp[1])) - jnp.exp(jnp.sum(lp[2] * lp[3])) + lam_init
    return lam, lam_init


def diff_attention_block(q, k, v, q_pos, k_pos, t5_table, lam, sub_gain, sub_scale):
    logits = jnp.einsum('bqhmd,bkhmd->bhmqk', q, k).astype(jnp.float32) * (HEAD_DIM ** -0.5)
    logits = logits + t5_bucket_bias(q_pos, k_pos, t5_table)[None, :, None]
    mask = (k_pos[None, :] // CHUNK) <= (q_pos[:, None] // CHUNK)
    logits = jnp.where(mask[None, None, None], logits, NEG_INF)
    p = jax.nn.softmax(logits, axis=-1)
    w = p[:, :, 0] - lam * p[:, :, 1]
    o = jnp.einsum('bhqk,bkhe->bqhe', w.astype(v.dtype), v)
    return rmsnorm(o, sub_gain) * sub_scale


def sweep_query_blocks(block_fn, q, k, v):
    b, s = q.shape[0], q.shape[1]
    nblk = s // QBLK
    q_blocks = jnp.moveaxis(q.reshape((b, nblk, QBLK) + q.shape[2:]), 1, 0)
    q_pos = jnp.arange(s, dtype=jnp.int32).reshape(nblk, QBLK)
    k_pos = jnp.arange(s, dtype=jnp.int32)
    out = lax.map(lambda qp: block_fn(qp[0], k, v, qp[1], k_pos), (q_blocks, q_pos))
    return jnp.moveaxis(out, 0, 1).reshape(b, s, -1)


def merge_branches(o_a, o_b, o_c, gate_pre, b_gate, w_branch, w_out):
    g = jax.nn.sigmoid(gate_pre.reshape(gate_pre.shape[:-1] + (N_BRANCH, D_MODEL)) + b_gate)
    h = (g[..., 0, :] * (o_a @ w_branch[0]) + g[..., 1, :] * (o_b @ w_branch[1])
         + g[..., 2, :] * (o_c @ w_branch[2]))
    return h @ w_out


def conv_ffn(xn, w_up, conv_w, conv_b, w_down, conv_state):
    gu = xn @ w_up
    g, u = gu[..., :D_FF], gu[..., D_FF:]
    t = g.shape[1]
    gp = jnp.concatenate([conv_state.astype(g.dtype), g], axis=1)
    c = conv_b + conv_w[0] * gp[:, 0:t]
    for kk in range(1, CONV_W):
        c = c + conv_w[kk] * gp[:, kk:kk + t]
    h = jax.nn.gelu(c) * u
    return h @ w_down, gp[:, gp.shape[1] - (CONV_W - 1):]


def trunk(x, caches, norm_mix, w_in, b_gate, a_rel_bias, t5_bias, c_lambda, c_subln,
          w_branch, w_out, norm_ffn, w_up, conv_w, conv_b, w_down, norm_final):
    b, t = x.shape[0], x.shape[1]
    new_states = [[] for _ in range(7)]
    for l in range(DEPTH):
        xn = rmsnorm(x, norm_mix[l])
        qa, ka, va, qb, kb, vb, qc, kc, vc, gate_pre = mixer_inputs(xn, w_in[l])
        lam, lam_init = diff_lambda(c_lambda[l], l)
        diff_fn = functools.partial(diff_attention_block, t5_table=t5_bias, lam=lam,
                                    sub_gain=c_subln[l], sub_scale=1.0 - lam_init)
        if caches is None:
            o_a = band_attention_prompt(qa, ka, va, a_rel_bias[l])
            o_b = sweep_query_blocks(stick_breaking_block, qb, kb, vb)
            o_c = sweep_query_blocks(diff_fn, qc, kc, vc)
            a_keep = min(A_PAST_CHUNKS * CHUNK, t)
            st_a_k, st_a_v = ka[:, t - a_keep:], va[:, t - a_keep:]
            conv_state = jnp.zeros((b, CONV_W - 1, D_FF), x.dtype)
        else:
            cache_a_k, cache_a_v, cache_b_k, cache_b_v, cache_c_k, cache_c_v, state_conv = caches
            past = cache_b_k.shape[2]
            q_pos = past + jnp.arange(t, dtype=jnp.int32)
            k_pos = jnp.arange(past + t, dtype=jnp.int32)
            o_a, st_a_k, st_a_v = band_attention_sample(qa, ka, va, cache_a_k[l], cache_a_v[l],
                                                        a_rel_bias[l], past)
            o_b = stick_breaking_block(qb, jnp.concatenate([cache_b_k[l], kb], axis=1),
                                       jnp.concatenate([cache_b_v[l], vb], axis=1),
                                       q_pos, k_pos).reshape(b, t, -1)
            o_c = diff_fn(qc, jnp.concatenate([cache_c_k[l], kc], axis=1),
                          jnp.concatenate([cache_c_v[l], vc], axis=1), q_pos, k_pos).reshape(b, t, -1)
            conv_state = state_conv[l]
        x = x + merge_branches(o_a, o_b, o_c, gate_pre, b_gate[l], w_branch[l], w_out[l])
        f, st_conv = conv_ffn(rmsnorm(x, norm_ffn[l]), w_up[l], conv_w[l], conv_b[l], w_down[l], conv_state)
        x = x + f
        for lst, s in zip(new_states, (st_a_k, st_a_v, kb, vb, kc, vc, st_conv)):
            lst.append(s)
    y = rmsnorm(x, norm_final)
    return y, tuple(jnp.stack(s, axis=0) for s in new_states)


def setup_inputs(seed: int = 0) -> dict:
    key = jax.random.key(seed)
    ks = jax.random.split(key, 24)
    a_len = min(A_PAST_CHUNKS * CHUNK, PAST_LEN)

    def nrm(k, shape, scale):
        return jax.random.normal(k, shape, jnp.float32) * scale

    return {
        'x_prompt': nrm(ks[0], (BATCH, SEQ, D_MODEL), 1.0),
        'x_sample': nrm(ks[1], (DEC_BATCH, DEC_SEQ, D_MODEL), 1.0),
        'cache_a_k': nrm(ks[2], (DEPTH, DEC_BATCH, a_len, A_HEADS, HEAD_DIM), 1.0),
        'cache_a_v': nrm(ks[3], (DEPTH, DEC_BATCH, a_len, A_HEADS, HEAD_DIM), 1.0),
        'cache_b_k': nrm(ks[4], (DEPTH, DEC_BATCH, PAST_LEN, B_HEADS, HEAD_DIM), 1.0),
        'cache_b_v': nrm(ks[5], (DEPTH, DEC_BATCH, PAST_LEN, B_HEADS, HEAD_DIM), 1.0),
        'cache_c_k': nrm(ks[6], (DEPTH, DEC_BATCH, PAST_LEN, C_HEADS, 2, HEAD_DIM), 1.0),
        'cache_c_v': nrm(ks[7], (DEPTH, DEC_BATCH, PAST_LEN, C_HEADS, C_VDIM), 1.0),
        'state_ffn_conv': nrm(ks[8], (DEPTH, DEC_BATCH, CONV_W - 1, D_FF), 1.0),
        'norm_mix': 1.0 + nrm(ks[9], (DEPTH, D_MODEL), 0.01),
        'w_in': nrm(ks[10], (DEPTH, D_MODEL, IN_COLS), D_MODEL ** -0.5),
        'b_gate': nrm(ks[11], (DEPTH, N_BRANCH, D_MODEL), 0.1),
        'a_rel_bias': nrm(ks[12], (DEPTH, 2 * A_REL_CLIP + 1, A_HEADS), 0.5),
        't5_bias': nrm(ks[13], (T5_BUCKETS, C_HEADS), 0.5),
        'c_lambda': nrm(ks[14], (DEPTH, 4, HEAD_DIM), 0.1),
        'c_subln': 1.0 + nrm(ks[15], (DEPTH, C_VDIM), 0.01),
        'w_branch': nrm(ks[16], (DEPTH, N_BRANCH, BRANCH_W, D_MODEL), BRANCH_W ** -0.5),
        'w_out': nrm(ks[17], (DEPTH, D_MODEL, D_MODEL), D_MODEL ** -0.5),
        'norm_ffn': 1.0 + nrm(ks[18], (DEPTH, D_MODEL), 0.01),
        'w_up': nrm(ks[19], (DEPTH, D_MODEL, 2 * D_FF), D_MODEL ** -0.5),
        'conv_w': nrm(ks[20], (DEPTH, CONV_W, D_FF), CONV_W ** -0.5),
        'conv_b': nrm(ks[21], (DEPTH, D_FF), 0.01),
        'w_down': nrm(ks[22], (DEPTH, D_FF, D_MODEL), D_FF ** -0.5),
        'norm_final': 1.0 + nrm(ks[23], (D_MODEL,), 0.01),
    }


def reference(x_prompt, x_sample, cache_a_k, cache_a_v, cache_b_k, cache_b_v, cache_c_k, cache_c_v,
              state_ffn_conv, norm_mix, w_in, b_gate, a_rel_bias, t5_bias, c_lambda, c_subln,
              w_branch, w_out, norm_ffn, w_up, conv_w, conv_b, w_down, norm_final):
    y_prompt, p_states = trunk(x_prompt, None, norm_mix, w_in, b_gate, a_rel_bias, t5_bias, c_lambda,
                               c_subln, w_branch, w_out, norm_ffn, w_up, conv_w, conv_b, w_down, norm_final)
    caches = (cache_a_k, cache_a_v, cache_b_k, cache_b_v, cache_c_k, cache_c_v, state_ffn_conv)
    y_sample, s_states = trunk(x_sample, caches, norm_mix, w_in, b_gate, a_rel_bias, t5_bias, c_lambda,
                               c_subln, w_branch, w_out, norm_ffn, w_up, conv_w, conv_b, w_down, norm_final)
    pa_k, pa_v, pb_k, pb_v, pc_k, pc_v, p_conv = p_states
    sa_k, sa_v, sb_k, sb_v, sc_k, sc_v, s_conv = s_states
    return (y_prompt, y_sample, pa_k, pa_v, pb_k, pb_v, pc_k, pc_v, p_conv,
            sa_k, sa_v, sb_k, sb_v, sc_k, sc_v, s_conv)
```

```python
import numpy as np
from contextlib import ExitStack

import concourse.bass as bass
import concourse.mybir as mybir
from concourse.bass_utils import run_bass_kernel_spmd

F32 = mybir.dt.float32
BF16 = mybir.dt.bfloat16
AF = mybir.ActivationFunctionType
ALU = mybir.AluOpType
AX = mybir.AxisListType

D = 1024
NBLK = 9
BT = 512
SEQ = 4096
DFF = 2816
NFC = 22
EPS = 1e-6
NEG = -30000.0
GELU_K = 1.5957691216057308


class Sched:
    ENG = ("pe", "act", "dve", "pool", "sp")

    def __init__(self, nc, es):
        self.nc = nc
        self.es = es
        self.ops = {e: [] for e in self.ENG}
        self.sem = {e: es.enter_context(nc.semaphore("s_" + e)) for e in self.ENG}
        self.cnt = {e: 0 for e in self.ENG}
        self.known = {e: {} for e in self.ENG}
        self.res_w = {}
        self.res_r = {}
        self.dma = {}
        self.nwait = 0

    def _wait(self, eng, tok):
        if tok is None:
            return
        key, h, v = tok
        if key == "pe" and eng == "pe":
            return
        if self.known[eng].get(key, 0) >= v:
            return
        self.known[eng][key] = v
        self.ops[eng].append(("wait", h, v))
        self.nwait += 1

    def op(self, eng, fn, reads=(), writes=(), dma=None):
        writes = list(writes) + [r for r in reads if r.startswith("ps")]
        reads = [r for r in reads if not r.startswith("ps")]
        if dma is not None:
            if dma not in self.dma:
                self.dma[dma] = [self.es.enter_context(self.nc.semaphore("d_" + dma.replace(".", "_"))), 0]
            writes.append("__dma." + dma)
        for r in reads:
            self._wait(eng, self.res_w.get(r))
        for w in writes:
            self._wait(eng, self.res_w.get(w))
            rr = self.res_r.get(w)
            if rr:
                for t in rr.values():
                    self._wait(eng, t)
        if dma is None:
            self.cnt[eng] += 1
            tok = (eng, self.sem[eng], self.cnt[eng])
            self.ops[eng].append(("op", fn, self.sem[eng], 1))
        else:
            d = self.dma[dma]
            d[1] += 16
            tok = (dma, d[0], d[1])
            self.ops[eng].append(("op", fn, d[0], 16))
        for r in reads:
            self.res_r.setdefault(r, {})[tok[0]] = tok
        for w in writes:
            self.res_w[w] = tok
            self.res_r[w] = {}
        return tok

    def barrier(self):
        toks = [(e, self.sem[e], self.cnt[e]) for e in self.ENG if self.cnt[e] > 0]
        toks += [(k, d[0], d[1]) for k, d in self.dma.items() if d[1] > 0]
        for e in self.ENG:
            for t in toks:
                if t[0] != e:
                    self._wait(e, t)
        self.res_w = {}
        self.res_r = {}

    def replay(self):
        with self.nc.Block() as block:
            self._replay(block)
        self.ops = {e: [] for e in self.ENG}

    def _replay(self, block):
        def run(e, handle):
            for it in self.ops[e]:
                if it[0] == "wait":
                    handle.wait_ge(it[1], it[2])
                else:
                    it[1](handle).then_inc(it[2], it[3])

        @block.tensor
        def _(h):
            run("pe", h)

        @block.scalar
        def _(h):
            run("act", h)

        @block.vector
        def _(h):
            run("dve", h)

        @block.gpsimd
        def _(h):
            run("pool", h)

        @block.sync
        def _(h):
            run("sp", h)


class Builder:
    def __init__(self, nphase=None):
        self.nc = bass.Bass("TRN2", target_bir_lowering=False)
        self.es = ExitStack()
        self.evac_rr = 0
        self.nphase = nphase

    def din(self, name, shape, dt=F32):
        return self.nc.dram_tensor(name, list(shape), dt, kind="ExternalInput").ap()

    def dout(self, name, shape, dt=F32):
        return self.nc.dram_tensor(name, list(shape), dt, kind="ExternalOutput").ap()

    def dscr(self, name, shape, dt):
        return self.nc.dram_tensor(name, list(shape), dt).ap()

    def sb(self, st, name, shape, dt):
        self.uid = getattr(self, "uid", 0) + 1
        return st.enter_context(self.nc.sbuf_tensor("%s_u%d" % (name, self.uid), list(shape), dt))

    def mm(self, out, lhsT, rhs, start, stop, reads, writes):
        self.S.op("pe", lambda h: h.matmul(out, lhsT=lhsT, rhs=rhs, start=start, stop=stop,
                                           skip_group_check=True), reads=reads, writes=writes)

    def act(self, out, in_, func, reads, writes, bias=None, scale=None, accum_out=None):
        kw = {}
        if bias is not None:
            kw["bias"] = bias
        if scale is not None:
            kw["scale"] = scale
        if accum_out is not None:
            kw["accum_out"] = accum_out
        self.S.op("act", lambda h: h.activation(out=out, in_=in_, func=func, **kw), reads=reads, writes=writes)

    def ts(self, eng, out, in0, s1, s2, op0, op1, reads, writes):
        if s2 is None:
            self.S.op(eng, lambda h: h.tensor_scalar(out=out, in0=in0, scalar1=s1, scalar2=None, op0=op0),
                      reads=reads, writes=writes)
        else:
            self.S.op(eng, lambda h: h.tensor_scalar(out=out, in0=in0, scalar1=s1, scalar2=s2, op0=op0, op1=op1),
                      reads=reads, writes=writes)

    def stt(self, eng, out, in0, scalar, in1, op0, op1, reads, writes):
        self.S.op(eng, lambda h: h.scalar_tensor_tensor(out=out, in0=in0, scalar=scalar, in1=in1, op0=op0, op1=op1),
                  reads=reads, writes=writes)

    def tt(self, eng, out, in0, in1, op, reads, writes):
        self.S.op(eng, lambda h: h.tensor_tensor(out=out, in0=in0, in1=in1, op=op), reads=reads, writes=writes)

    def cp(self, eng, out, in_, reads, writes):
        if eng == "act":
            self.S.op("act", lambda h: h.activation(out=out, in_=in_, func=AF.Copy), reads=reads, writes=writes)
        else:
            self.S.op(eng, lambda h: h.tensor_copy(out=out, in_=in_), reads=reads, writes=writes)

    def memset(self, eng, ap, val, writes):
        self.S.op(eng, lambda h: h.memset(ap, val), writes=writes)

    def dma(self, eng, out, in_, reads, writes, sem, slow=False):
        if slow:
            self.S.op(eng, lambda h: h.dma_start(out=out, in_=in_, allow_slow_non_contiguous=True), reads=reads,
                      writes=writes, dma=sem)
        else:
            self.S.op(eng, lambda h: h.dma_start(out=out, in_=in_), reads=reads, writes=writes, dma=sem)

    def evac_eng(self):
        self.evac_rr += 1
        return "act" if self.evac_rr % 2 else "dve"

    def declare(self):
        nc = self.nc
        I = {}
        I["xp"] = self.din("xp", [SEQ, D])
        I["xs"] = self.din("xs", [BT, D])
        I["cak"] = self.din("cak", [2, 4, 512, 512])
        I["cav"] = self.din("cav", [2, 4, 512, 512])
        I["cbk"] = self.din("cbk", [2, 4, 1024, 512])
        I["cbv"] = self.din("cbv", [2, 4, 1024, 512])
        I["cck"] = self.din("cck", [2, 4, 1024, 512])
        I["ccv"] = self.din("ccv", [2, 4, 1024, 512])
        I["cst"] = self.din("cst", [128, 2, NFC, 4, 2])
        I["w_in"] = self.din("w_in", [2, D, 7680])
        I["w_branch"] = self.din("w_branch", [2, 3, 512, D])
        I["w_out"] = self.din("w_out", [2, D, D])
        I["w_up"] = self.din("w_up", [2, D, 2 * DFF])
        I["w_down"] = self.din("w_down", [2, DFF, D])
        I["gains"] = self.din("gains", [5, 128, D])
        I["bgate"] = self.din("bgate", [128, 2, 3, 8])
        I["convw"] = self.din("convw", [128, 2, 3, NFC])
        I["convb"] = self.din("convb", [128, 2, NFC])
        I["lam"] = self.din("lam", [128, 2, 256])
        I["subln"] = self.din("subln", [128, 2, 128])
        I["sublnT"] = self.din("sublnT", [128, 2])
        I["tza"] = self.din("tza", [2, 128, 8, 640])
        I["tzc"] = self.din("tzc", [128, 4, 640])
        I["c15"] = self.din("c15", [128, 4])
        I["mk01"] = self.din("mk01", [128, 512])
        I["mkneg"] = self.din("mkneg", [128, 512])
        self.I = I
        O = {}
        O["yp"] = self.dout("yp", [SEQ, D])
        O["ys"] = self.dout("ys", [4, 64, D])
        O["pak"] = self.dout("pak", [2, 512, 512])
        O["pav"] = self.dout("pav", [2, 512, 512])
        for n in ("pbk", "pbv", "pck", "pcv"):
            O[n] = self.dout(n, [2, SEQ, 512])
        O["pconv"] = self.dout("pconv", [2, 2, DFF])
        O["sak"] = self.dout("sak", [2, 4, 512, 512])
        O["sav"] = self.dout("sav", [2, 4, 512, 512])
        for n in ("sbk", "sbv", "sck", "scv"):
            O[n] = self.dout(n, [2, 4, 64, 512])
        O["sconv"] = self.dout("sconv", [2, 4, 2, DFF])
        self.O = O
        self.xres = self.dscr("xres", [NBLK * BT, D], F32)
        self.xnT_d = self.dscr("xnT_d", [NBLK, 128, 8, BT], BF16)
        self.oT_d = [self.dscr("oT_d%d" % i, [NBLK, 128, 4, BT], BF16) for i in range(3)]

    def build(self):
        nc = self.nc
        es = self.es
        self.declare()
        I = self.I
        self.S = Sched(nc, es)
        S = self.S
        self.ps = [es.enter_context(nc.psum_tensor("psb%d" % i, [128, 512], F32)) for i in range(8)]
        g = es
        self.ident = self.sb(g, "ident", [128, 128], BF16)
        self.negtri = self.sb(g, "negtri", [128, 128], BF16)
        self.negones = self.sb(g, "negones", [128, 128], BF16)
        self.posones = self.sb(g, "posones", [128, 128], BF16)
        self.sublnT = self.sb(g, "sublnT", [128, 2], F32)
        self.bgate = self.sb(g, "bgates", [128, 2, 3, 8], F32)
        self.convw = self.sb(g, "convws", [128, 2, 3, NFC], F32)
        self.convb = self.sb(g, "convbs", [128, 2, NFC], F32)
        self.cst = self.sb(g, "csts", [128, 2, NFC, 4, 2], F32)
        self.lam = self.sb(g, "lams", [128, 2, 256], F32)
        self.subln = self.sb(g, "sublns", [128, 2, 128], F32)
        self.lamw = self.sb(g, "lamw", [128, 2, 2, 64], F32)
        self.lams = self.sb(g, "lamsm", [128, 2, 8], F32)
        self.neglam = self.sb(g, "neglam", [128, 2], F32)
        tmpf = self.sb(g, "ctmpf", [128, 128], F32)

        self.memset("pool", tmpf[:], 0.0, ["tmpf"])
        S.op("pool", lambda h: h.affine_select(out=tmpf[:], in_=tmpf[:], pattern=[[-1, 128]],
                                               compare_op=ALU.not_equal, fill=1.0, base=0, channel_multiplier=1),
             reads=["tmpf"], writes=["tmpf"])
        self.cp("dve", self.ident[:], tmpf[:], ["tmpf"], ["ident"])
        self.memset("pool", tmpf[:], -1.0, ["tmpf"])
        S.op("pool", lambda h: h.affine_select(out=tmpf[:], in_=tmpf[:], pattern=[[-1, 128]],
                                               compare_op=ALU.is_ge, fill=0.0, base=0, channel_multiplier=1),
             reads=["tmpf"], writes=["tmpf"])
        self.cp("dve", self.negtri[:], tmpf[:], ["tmpf"], ["negtri"])
        self.memset("pool", self.negones[:], -1.0, ["negones"])
        self.memset("pool", self.posones[:], 1.0, ["posones"])
        self.dma("sp", self.sublnT[:], I["sublnT"], [], ["sublnT"], "c3")
        self.dma("sp", self.bgate[:], I["bgate"], [], ["bgate"], "c4")
        self.dma("sp", self.convw[:], I["convw"], [], ["convw"], "c5")
        self.dma("sp", self.convb[:], I["convb"], [], ["convb"], "c6")
        self.dma("sp", self.cst[:], I["cst"], [], ["cst"], "c7")
        self.dma("sp", self.lam[:], I["lam"], [], ["lam"], "c8")
        self.dma("sp", self.subln[:], I["subln"], [], ["subln"], "c9")
        for l in range(2):
            lv = self.lam[:, l, :].rearrange("p (a b c) -> p a b c", a=2, b=2, c=64)
            self.tt("dve", self.lamw[:, l, :, :], lv[:, :, 0, :], lv[:, :, 1, :], ALU.mult, ["lam"], ["lamw"])
            S.op("dve", lambda h, l=l: h.reduce_sum(out=self.lams[:, l, 0:2], in_=self.lamw[:, l, :, :], axis=AX.X),
                 reads=["lamw"], writes=["lams"])
            self.act(self.lams[:, l, 2:4], self.lams[:, l, 0:2], AF.Exp, ["lams"], ["lams"])
            self.tt("dve", self.lams[:, l, 4:5], self.lams[:, l, 3:4], self.lams[:, l, 2:3], ALU.subtract,
                    ["lams"], ["lams"])
            lam_init = 0.8 - 0.6 * float(np.exp(-0.3 * l))
            self.ts("dve", self.neglam[:, l:l + 1], self.lams[:, l, 4:5], -lam_init, None, ALU.add, None,
                    ["lams"], ["neglam"])
            self.ts("dve", self.sublnT[:, l:l + 1], self.sublnT[:, l:l + 1], 1.0 - lam_init, None, ALU.mult, None,
                    ["sublnT"], ["sublnT"])
        S.barrier()
        S.replay()

        def layer(l):
            with ExitStack() as c1:
                wB = self.pre_attn(c1, l, "B")
                with ExitStack() as c2:
                    wA = self.pre_attn(c2, l, "A")
                    self.phase_N(l, 0)
                    self.phase_attn(l, "A", wA)
                self.phase_attn(l, "B", wB)
            self.phase_attn(l, "C")
            self.phase_M(l)
            with ExitStack() as c3:
                wts = self.pre_F(c3, l)
                self.phase_N(l, 1)
                self.phase_F(l, wts)

        if self.nphase is None:
            for l in range(2):
                layer(l)
            self.phase_N(2, 2)
        else:
            plist = []
            for l in range(2):
                plist += [lambda l=l: self.phase_N(l, 0), lambda l=l: self.phase_attn(l, "A"),
                          lambda l=l: self.phase_attn(l, "B"), lambda l=l: self.phase_attn(l, "C"),
                          lambda l=l: self.phase_M(l), lambda l=l: self.phase_N(l, 1), lambda l=l: self.phase_F(l)]
            plist.append(lambda: self.phase_N(2, 2))
            for p in plist[:self.nphase]:
                p()
        es.close()
        return nc

    def x_tile_src(self, from_input, blk, ti):
        if from_input:
            if blk < 8:
                return self.I["xp"][blk * BT + ti * 128: blk * BT + (ti + 1) * 128, :]
            return self.I["xs"][ti * 128:(ti + 1) * 128, :]
        r0 = blk * BT + ti * 128
        return self.xres[r0:r0 + 128, :]

    def phase_N(self, l, kind):
        S = self.S
        with ExitStack() as ph:
            gain = self.sb(ph, "n_gain", [128, D], F32)
            NS = 4
            xt = [self.sb(ph, "n_xt%d" % i, [128, D], F32) for i in range(NS)]
            junk = self.sb(ph, "n_junk", [128, D], F32)
            xn = [self.sb(ph, "n_xn%d" % i, [128, D], BF16) for i in range(2)]
            yt = [self.sb(ph, "n_yt%d" % i, [128, D], F32) for i in range(2)]
            st = [self.sb(ph, "n_st%d" % i, [128, 4], F32) for i in range(NS)]
            xnT = [self.sb(ph, "n_xnT%d" % i, [128, 8, BT], BF16) for i in range(2)]
            gi = {0: 2 * l, 1: 2 * l + 1, 2: 4}[kind]
            self.dma("sp", gain[:], self.I["gains"][gi], [], ["n_gain"], "n_g")
            from_input = (kind == 0 and l == 0)
            tiles = [(blk, ti) for blk in range(NBLK) for ti in range(4)]

            def sa(k):
                blk, ti = tiles[k]
                s = k % NS
                self.dma("sp", xt[s][:], self.x_tile_src(from_input, blk, ti), ["xres.%d.%d" % (blk, ti)],
                         ["n_xt%d" % s], "n_x%d" % s)
                self.memset("dve", st[s][:, 0:1], 0.0, ["n_st%d" % s])
                self.act(junk[:], xt[s][:], AF.Square, ["n_xt%d" % s, "n_st%d" % s], ["n_junk", "n_st%d" % s],
                         accum_out=st[s][:, 0:1])
                self.ts("dve", st[s][:, 1:2], st[s][:, 0:1], 1.0 / D, EPS, ALU.mult, ALU.add,
                        ["n_st%d" % s], ["n_st%d" % s])

            def sb_(k):
                blk, ti = tiles[k]
                s = k % NS
                s2 = k % 2
                bs = blk % 2
                self.act(st[s][:, 2:3], st[s][:, 1:2], AF.Ln, ["n_st%d" % s], ["n_st%d" % s])
                self.act(st[s][:, 3:4], st[s][:, 2:3], AF.Exp, ["n_st%d" % s], ["n_st%d" % s], scale=-0.5)
                if kind == 2:
                    self.stt("dve", yt[s2][:], xt[s][:], st[s][:, 3:4], gain[:], ALU.mult, ALU.mult,
                             ["n_xt%d" % s, "n_st%d" % s, "n_gain"], ["n_yt%d" % s2])
                    if blk < 8:
                        self.dma("sp", self.O["yp"][blk * BT + ti * 128: blk * BT + (ti + 1) * 128, :], yt[s2][:],
                                 ["n_yt%d" % s2], [], "n_y%d" % s2)
                    else:
                        self.dma("sp", self.O["ys"][ti], yt[s2][0:64, :], ["n_yt%d" % s2], [], "n_y%d" % s2)
                    return
                self.stt("dve", xn[s2][:], xt[s][:], st[s][:, 3:4], gain[:], ALU.mult, ALU.mult,
                         ["n_xt%d" % s, "n_st%d" % s, "n_gain"], ["n_xn%d" % s2])
                pb = s2
                psT = self.ps[pb][:].bitcast(BF16)
                for kc in range(8):
                    S.op("pe", lambda h, kc=kc, psT=psT, s2=s2: h.transpose(
                        out=psT[:, kc * 128:(kc + 1) * 128], in_=xn[s2][:, kc * 128:(kc + 1) * 128],
                        identity=self.ident[:]), reads=["n_xn%d" % s2, "ident"], writes=["ps%d" % pb])
                self.cp(self.evac_eng(), xnT[bs][:, :, ti * 128:(ti + 1) * 128],
                        psT.rearrange("p (k t) -> p k t", k=8), ["ps%d" % pb], ["n_xnT%d" % bs])
                if ti == 3:
                    self.dma("sp", self.xnT_d[blk], xnT[bs][:], ["n_xnT%d" % bs], ["xnT_d.%d" % blk], "n_o%d" % bs)

            nt = len(tiles)
            sa(0)
            sa(1)
            for k in range(nt):
                if k + 2 < nt:
                    sa(k + 2)
                sb_(k)
            S.barrier()
            S.replay()

    def load_w(self, dst, src, name, nsplit, sem):
        kcn = dst.shape[1]
        for kc in range(kcn):
            self.dma("pool", dst[:, kc, :], src[kc * 128:(kc + 1) * 128, :], [], ["%s.%d" % (name, kc)],
                     "%s%d" % (sem, kc % nsplit))

    def proj_fm(self, w, wname, col0, xnT, xname, bank, nk=8):
        for kc in range(nk):
            self.mm(self.ps[bank][:], w[:, kc, col0:col0 + 128], xnT[:, kc, :], kc == 0, kc == nk - 1,
                    ["%s.%d" % (wname, kc), xname], ["ps%d" % bank])

    def proj_tm(self, w, wname, col0, xnT, xname, ti, bank, nk=8):
        for kc in range(nk):
            self.mm(self.ps[bank][:], xnT[:, kc, ti * 128:(ti + 1) * 128], w[:, kc, col0:col0 + 512], kc == 0,
                    kc == nk - 1, ["%s.%d" % (wname, kc), xname], ["ps%d" % bank])

    def pre_attn(self, st, l, br):
        col0 = "ABC".index(br) * 1536
        w = self.sb(st, "a_w", [128, 8, 1536], BF16)
        self.load_w(w, self.I["w_in"][l][:, col0:col0 + 1536], "a_w", 4, "a_w" + br)
        return w

    def pre_F(self, st, l):
        wU = self.sb(st, "f_wu", [128, 8, 2 * DFF], BF16)
        wD = self.sb(st, "f_wd", [128, NFC, D], BF16)
        self.load_w(wU, self.I["w_up"][l], "f_wu", 4, "f_wu")
        self.load_w(wD, self.I["w_down"][l], "f_wd", 4, "f_wd")
        return wU, wD

    def phase_attn(self, l, br, w=None):
        S = self.S
        I, O = self.I, self.O
        bi = "ABC".index(br)
        col0 = bi * 1536
        with ExitStack() as ph:
            if w is None:
                w = self.pre_attn(ph, l, br)
            xnT = [self.sb(ph, "a_xnT%d" % i, [128, 8, BT], BF16) for i in range(2)]
            Qz = [self.sb(ph, "a_qz%d" % i, [128, BT], BF16) for i in range(8)]
            stg = [self.sb(ph, "a_stg%d" % i, [128, 512], F32) for i in range(4)]
            oT = [self.sb(ph, "a_oT%d" % i, [128, 4, BT], BF16) for i in range(2)]
            kctok = [self.sb(ph, "a_kct%d" % i, [128, 8, 512], BF16) for i in range(2)]
            KTn = self.sb(ph, "a_ktn", [128, 4, BT], BF16)
            for i in range(8):
                self.memset("pool", Qz[i][:], 0.0, ["a_qz%d" % i])
            for i in range(2):
                self.memset("pool", oT[i][:], 0.0, ["a_oT%d" % i])
            B = {}
            if br == "A":
                tza = self.sb(ph, "a_tza", [128, 8, 640], BF16)
                self.dma("pool", tza[:], I["tza"][l], [], ["a_tza"], "a_tz")
                KT = self.sb(ph, "a_kt", [128, 4, 1024], BF16)
                V = self.sb(ph, "a_v", [128, 8, 8, 65], BF16)
                Vn = self.sb(ph, "a_vn", [128, 4, 8, 65], BF16)
                self.memset("pool", V[:], 1.0, ["a_v.%d" % i for i in range(8)])
                self.memset("pool", Vn[:], 1.0, ["a_vn.%d" % i for i in range(4)])
                PT = [self.sb(ph, "a_pt%d" % i, [128, BT], BF16) for i in range(4)]
                otok = self.sb(ph, "a_otok", [128, 4, 512], BF16)
                dd = self.sb(ph, "a_dd", [128, 2, 8], F32)
                B.update(tza=tza, KT=KT, V=V, Vn=Vn, PT=PT, otok=otok, dd=dd)
            elif br == "B":
                KT = self.sb(ph, "a_kt", [128, 4, SEQ], BF16)
                V = self.sb(ph, "a_v", [128, 32, 512], BF16)
                Vn = self.sb(ph, "a_vn", [128, 4, 512], BF16)
                G = 16
                LP = [self.sb(ph, "b_lp%d" % i, [128, BT], BF16) for i in range(2 * G)]
                LS = [self.sb(ph, "b_ls%d" % i, [128, BT], BF16) for i in range(2)]
                WT = [self.sb(ph, "b_wt%d" % i, [128, BT], BF16) for i in range(4)]
                self.mk01 = self.sb(ph, "mk01s", [128, 512], BF16)
                self.mkneg = self.sb(ph, "mknegs", [128, 512], BF16)
                self.dma("pool", self.mk01[:], I["mk01"], [], ["mk01"], "c0")
                self.dma("pool", self.mkneg[:], I["mkneg"], [], ["mkneg"], "c1")
                B.update(KT=KT, V=V, Vn=Vn, G=G, LP=LP, LS=LS, WT=WT)
            else:
                KT = self.sb(ph, "a_kt", [128, 4, SEQ], BF16)
                V = self.sb(ph, "a_v", [128, 32, 512], BF16)
                Vn = self.sb(ph, "a_vn", [128, 4, 512], BF16)
                PT = [self.sb(ph, "a_pt%d" % i, [128, BT], BF16) for i in range(4)]
                PA = [self.sb(ph, "c_pa%d" % i, [128, BT], F32) for i in range(2)]
                cw = dict(pab=[self.sb(ph, "c_pab%d" % i, [128, BT], BF16) for i in range(2)],
                          r=[self.sb(ph, "c_r%d" % i, [128, BT], F32) for i in range(2)],
                          o32=self.sb(ph, "c_o32", [128, BT], F32), sq=self.sb(ph, "c_sq", [128, BT], BF16),
                          os=[self.sb(ph, "c_os%d" % i, [128, BT], F32) for i in range(2)])
                self.tzc = self.sb(ph, "tzcs", [128, 4, 640], BF16)
                self.c15 = self.sb(ph, "c15s", [128, 4], F32)
                self.dma("pool", self.tzc[:], I["tzc"], [], ["tzc"], "c2")
                self.dma("sp", self.c15[:], I["c15"], [], ["c15"], "c3")
                B.update(KT=KT, V=V, Vn=Vn, PT=PT, PA=PA, cw=cw)
            B.update(w=w, xnT=xnT, Qz=Qz, stg=stg, oT=oT, kctok=kctok, KTn=KTn, l=l, br=br)
            self.stg_i = 0
            self.dma("sp", xnT[0][:], self.xnT_d[0], ["xnT_d.0"], ["a_xnT0"], "a_x0")
            for blk in range(NBLK):
                xs_ = blk % 2
                if blk + 1 < NBLK:
                    self.dma("sp", xnT[1 - xs_][:], self.xnT_d[blk + 1], ["xnT_d.%d" % (blk + 1)],
                             ["a_xnT%d" % (1 - xs_)], "a_x%d" % (1 - xs_))
                if "blocks" in _DEBUG and blk not in _DEBUG["blocks"]:
                    continue
                self.attn_block(B, blk)
            S.barrier()
            S.replay()

    def stage_out(self, B, bank, dsts):
        i = self.stg_i % 4
        self.stg_i += 1
        st = B["stg"][i]
        self.cp(self.evac_eng(), st[:], self.ps[bank][:], ["ps%d" % bank], ["a_stg%d" % i])
        for (dst, npart) in dsts:
            self.dma("sp", dst, st[0:npart, :], ["a_stg%d" % i], [], "a_so%d" % i)

    def attn_block(self, B, blk):
        S = self.S
        I, O = self.I, self.O
        l, br = B["l"], B["br"]
        xs_ = blk % 2
        xnT = B["xnT"][xs_]
        xname = "a_xnT%d" % xs_
        w = B["w"]
        Qz = B["Qz"]
        sample = (blk == 8)
        pbank = [0, 1]
        pk = 0
        if sample:
            self.load_cache(B, 0)
            self.load_cache(B, 1)
        for oc in range(4):
            bk = pbank[pk % 2]
            pk += 1
            self.proj_fm(w, "a_w", oc * 128, xnT, xname, bk)
            self.act(Qz[2 * oc][0:64, :], self.ps[bk][0:64, :], AF.Copy, ["ps%d" % bk], ["a_qz%d" % (2 * oc)],
                     scale=0.125)
            self.ts("dve", Qz[2 * oc + 1][64:128, :], self.ps[bk][64:128, :], 0.125, None, ALU.mult, None,
                    ["ps%d" % bk], ["a_qz%d" % (2 * oc + 1)])
        for oc in range(4):
            bk = pbank[pk % 2]
            pk += 1
            self.proj_fm(w, "a_w", 512 + oc * 128, xnT, xname, bk)
            if sample:
                dst, dn = B["KTn"][:, oc, :], "a_ktn"
            elif br == "A":
                dst, dn = B["KT"][:, oc, (blk % 2) * 512:(blk % 2) * 512 + 512], "a_kt.%d" % (blk % 2)
            else:
                dst, dn = B["KT"][:, oc, blk * 512:(blk + 1) * 512], "a_kt.%d" % blk
            self.cp(self.evac_eng(), dst, self.ps[bk][:], ["ps%d" % bk], [dn])
        kname = {"A": ("pak", "sak"), "B": ("pbk", "sbk"), "C": ("pck", "sck")}[br]
        vname = {"A": ("pav", "sav"), "B": ("pbv", "sbv"), "C": ("pcv", "scv")}[br]
        for ti in range(4):
            need_k = sample or br != "A" or blk == 7
            if need_k:
                bk = pbank[pk % 2]
                pk += 1
                self.proj_tm(w, "a_w", 512, xnT, xname, ti, bk)
                if sample:
                    if br == "A":
                        d = [(O[kname[1]][l, ti, 448:512, :], 64)]
                    else:
                        d = [(O[kname[1]][l, ti], 64)]
                elif br == "A":
                    d = [(O[kname[0]][l, ti * 128:(ti + 1) * 128, :], 128)]
                else:
                    r0 = blk * BT + ti * 128
                    d = [(O[kname[0]][l, r0:r0 + 128, :], 128)]
                self.stage_out(B, bk, d)
            bk = pbank[pk % 2]
            pk += 1
            self.proj_tm(w, "a_w", 1024, xnT, xname, ti, bk)
            if br == "A":
                if sample:
                    vd, vn = B["Vn"][:, ti, :, 0:64], "a_vn.%d" % ti
                else:
                    vt = (blk % 2) * 4 + ti
                    vd, vn = B["V"][:, vt, :, 0:64], "a_v.%d" % vt
                src = self.ps[bk][:].rearrange("p (h d) -> p h d", h=8)
            elif br == "B":
                if sample:
                    vd, vn = B["Vn"][:, ti, :], "a_vn.%d" % ti
                else:
                    vd, vn = B["V"][:, blk * 4 + ti, :], "a_v.%d" % (blk * 4 + ti)
                src = self.ps[bk][:]
            else:
                if sample:
                    vd, vn = B["Vn"][:, ti, :], "a_vn.%d" % ti
                else:
                    vd, vn = B["V"][:, blk * 4 + ti, :], "a_v.%d" % (blk * 4 + ti)
                src = self.ps[bk][:]
            self.cp(self.evac_eng(), vd, src, ["ps%d" % bk], [vn])
            need_v = sample or br != "A" or blk == 7
            if need_v:
                if sample:
                    if br == "A":
                        d = [(O[vname[1]][l, ti, 448:512, :], 64)]
                    else:
                        d = [(O[vname[1]][l, ti], 64)]
                elif br == "A":
                    d = [(O[vname[0]][l, ti * 128:(ti + 1) * 128, :], 128)]
                else:
                    r0 = blk * BT + ti * 128
                    d = [(O[vname[0]][l, r0:r0 + 128, :], 128)]
                self.stage_out(B, bk, d)
        if _DEBUG.get("skip_attn"):
            return
        ob = blk % 2
        oT = B["oT"][ob]
        oname = "a_oT%d" % ob
        if not sample:
            if br == "A":
                self.attn_A(B, blk, None, oT, oname)
            elif br == "B":
                self.attn_B(B, blk, None, oT, oname)
            else:
                self.attn_C(B, blk, None, oT, oname)
        else:
            for s in range(4):
                if s >= 2:
                    self.load_cache(B, s)
                self.load_cache_tr(B, s)
                if _DEBUG.get("skip_sattn"):
                    continue
                if br == "A":
                    self.attn_A(B, blk, s, oT, oname)
                elif br == "B":
                    self.attn_B(B, blk, s, oT, oname)
                else:
                    self.attn_C(B, blk, s, oT, oname)
        bi = "ABC".index(br)
        self.dma("sp", self.oT_d[bi][blk], oT[:], [oname], ["oT_d%d.%d" % (bi, blk)], "a_oo%d" % ob)

    def load_cache(self, B, s):
        S = self.S
        I, O = self.I, self.O
        l, br = B["l"], B["br"]
        ntile = 4 if br == "A" else 8
        ck = I[{"A": "cak", "B": "cbk", "C": "cck"}[br]][l, s]
        cv = I[{"A": "cav", "B": "cbv", "C": "ccv"}[br]][l, s]
        reg = s % 2
        kct = B["kctok"][reg]
        self.dma("pool", kct[:, 0:ntile, :], ck.rearrange("(t p) c -> p t c", p=128), [], ["a_kct%d" % reg],
                 "a_ck%d" % reg)
        vt0 = reg * ntile
        for t in range(ntile):
            rows = cv[t * 128:(t + 1) * 128, :]
            if br == "A":
                vdst = B["V"][:, vt0 + t, :, 0:64]
                vsrc = rows.rearrange("p (h d) -> p h d", h=8)
            else:
                vdst = B["V"][:, vt0 + t, :]
                vsrc = rows
            self.dma("pool", vdst, vsrc, [], ["a_v.%d" % (vt0 + t)], "a_cv%d" % (t % 2))
        if br == "A":
            self.dma("sp", O["sak"][l, s, 0:448, :], ck[64:512, :], [], [], "a_dd0")
            self.dma("sp", O["sav"][l, s, 0:448, :], cv[64:512, :], [], [], "a_dd1")

    def load_cache_tr(self, B, s):
        S = self.S
        l, br = B["l"], B["br"]
        ntile = 4 if br == "A" else 8
        reg = s % 2
        kct = B["kctok"][reg]
        kt0 = reg * ntile * 128
        nb = 0
        for oc in range(4):
            for t0 in range(0, ntile, 4):
                bk = nb % 2
                nb += 1
                psT = self.ps[bk][:].bitcast(BF16)
                for t in range(4):
                    S.op("pe", lambda h, t=t, t0=t0, oc=oc, psT=psT: h.transpose(
                        out=psT[:, t * 128:(t + 1) * 128], in_=kct[:, t0 + t, oc * 128:(oc + 1) * 128],
                        identity=self.ident[:]), reads=["a_kct%d" % reg, "ident"], writes=["ps%d" % bk])
                c0 = kt0 + t0 * 128
                if br == "A":
                    names = ["a_kt.%d" % reg]
                else:
                    names = ["a_kt.%d" % ((c0 // 512) + i) for i in range(1)]
                self.cp(self.evac_eng(), B["KT"][:, oc, c0:c0 + 512], psT[:, 0:512], ["ps%d" % bk], names)

    def key_tiles(self, B, blk, s):
        br = B["br"]
        tiles = []
        if s is None:
            if br == "A":
                for t in range(8):
                    jb = blk - 1 if t < 4 else blk
                    if jb < 0:
                        continue
                    pos = jb % 2
                    c0 = pos * 512 + (t % 4) * 128
                    tiles.append(dict(kT=(lambda oc, c0=c0: B["KT"][:, oc, c0:c0 + 128]), kname="a_kt.%d" % pos,
                                      vi=pos * 4 + t % 4, new=False, nk=128, t=t))
            else:
                for tt in range(4 * blk + 4):
                    tiles.append(dict(kT=(lambda oc, tt=tt: B["KT"][:, oc, tt * 128:(tt + 1) * 128]),
                                      kname="a_kt.%d" % (tt // 4), vi=tt, new=False, nk=128, trel=tt - 4 * blk))
        else:
            reg = s % 2
            ntile = 4 if br == "A" else 8
            for t in range(ntile):
                c0 = reg * ntile * 128 + t * 128
                kn = "a_kt.%d" % reg if br == "A" else "a_kt.%d" % (c0 // 512)
                d = dict(kT=(lambda oc, c0=c0: B["KT"][:, oc, c0:c0 + 128]), kname=kn, vi=reg * ntile + t,
                         new=False, nk=128)
                if br == "A":
                    d["t"] = t
                else:
                    d["trel"] = t - 8
                tiles.append(d)
            d = dict(kT=(lambda oc: B["KTn"][:, oc, s * 128:s * 128 + 128]), kname="a_ktn", vi=s, new=True, nk=128)
            if br == "A":
                d["t"] = 4
            else:
                d["trel"] = 0
            tiles.append(d)
        return tiles

    def attn_A(self, B, blk, s, oT, oname):
        S = self.S
        l = B["l"]
        tiles = self.key_tiles(B, blk, s)
        Qz, PT, otok, dd, tza = B["Qz"], B["PT"], B["otok"], B["dd"], B["tza"]
        qb = 0 if s is None else s * 128
        nu = 4 if s is None else 1
        for pair in range(4):
            heads = (2 * pair, 2 * pair + 1)
            obase = 6 if pair % 2 == 0 else 0
            steps = []
            for td in tiles:
                t = td["t"]
                if s is None:
                    u0, u1 = max(0, t - 4), min(3, t)
                else:
                    u0, u1 = 0, 0
                steps.append((td, u0, u1))

            def emit_qk(si):
                td, u0, u1 = steps[si]
                nk = td["nk"]
                q0, q1 = qb + u0 * 128, qb + (u1 + 1) * 128
                c0 = (u0 - td["t"] + 4) * 128
                for ln in range(2):
                    hh = heads[ln]
                    bk = 2 + ln * 2 + si % 2
                    self.mm(self.ps[bk][0:nk, 0:q1 - q0], td["kT"](pair), Qz[hh][:, q0:q1], True, False,
                            [td["kname"], "a_qz%d" % hh], ["ps%d" % bk])
                    self.mm(self.ps[bk][0:nk, 0:q1 - q0], self.ident[0:nk, 0:nk], tza[0:nk, hh, c0:c0 + (q1 - q0)],
                            False, True, ["ident", "a_tza"], ["ps%d" % bk])

            first = [True, True]

            def emit_pv(si):
                td, u0, u1 = steps[si]
                nk = td["nk"]
                for ln in range(2):
                    hh = heads[ln]
                    pt = ln * 2 + si % 2
                    ob = obase + ln
                    if td["new"]:
                        vap, vn = B["Vn"][0:nk, td["vi"], hh, :], "a_vn.%d" % td["vi"]
                    else:
                        vap, vn = B["V"][0:nk, td["vi"], hh, :], "a_v.%d" % td["vi"]
                    for u in range(u0, u1 + 1):
                        self.mm(self.ps[ob][:, u * 65:(u + 1) * 65], PT[pt][0:nk, (u - u0) * 128:(u - u0 + 1) * 128],
                                vap, first[ln], False, ["a_pt%d" % pt, vn], ["ps%d" % ob])
                        first[ln] = False

            emit_qk(0)
            for si in range(len(steps)):
                if si + 1 < len(steps):
                    emit_qk(si + 1)
                td, u0, u1 = steps[si]
                nk = td["nk"]
                n = (u1 - u0 + 1) * 128
                for ln in range(2):
                    bk = 2 + ln * 2 + si % 2
                    pt = ln * 2 + si % 2
                    self.act(PT[pt][0:nk, 0:n], self.ps[bk][0:nk, 0:n], AF.Exp, ["ps%d" % bk], ["a_pt%d" % pt])
                if si > 0:
                    emit_pv(si - 1)
            emit_pv(len(steps) - 1)
            for ln in range(2):
                hh = heads[ln]
                ob = obase + ln
                den = self.ps[ob][:, 0:nu * 65].rearrange("p (u c) -> p u c", c=65)[:, :, 64]
                self.ts("dve", dd[:, ln, 0:nu], den, 1e-30, None, ALU.max, None, ["ps%d" % ob], ["a_dd"])
                S.op("dve", lambda h, ln=ln: h.reciprocal(out=dd[:, ln, 4:4 + nu], in_=dd[:, ln, 0:nu]),
                     reads=["a_dd"], writes=["a_dd"])
                for u in range(nu):
                    uu = u if s is None else s
                    self.ts("dve", otok[:, uu, hh * 64:(hh + 1) * 64], self.ps[ob][:, u * 65:u * 65 + 64],
                            dd[:, ln, 4 + u:5 + u], None, ALU.mult, None, ["ps%d" % ob, "a_dd"], ["a_otok"])
        self.otok_to_oT(B, s, oT, oname)

    def otok_to_oT(self, B, s, oT, oname):
        S = self.S
        otok = B["otok"]
        us = range(4) if s is None else [s]
        for kc in range(4):
            bk = kc % 2
            psT = self.ps[bk][:].bitcast(BF16)
            for u in us:
                S.op("pe", lambda h, u=u, kc=kc, psT=psT: h.transpose(
                    out=psT[:, u * 128:(u + 1) * 128], in_=otok[:, u, kc * 128:(kc + 1) * 128],
                    identity=self.ident[:]), reads=["a_otok", "ident"], writes=["ps%d" % bk])
            if s is None:
                self.cp(self.evac_eng(), oT[:, kc, :], psT[:, 0:512], ["ps%d" % bk], [oname])
            else:
                self.cp(self.evac_eng(), oT[:, kc, s * 128:(s + 1) * 128], psT[:, s * 128:(s + 1) * 128],
                        ["ps%d" % bk], [oname])

    def attn_B(self, B, blk, s, oT, oname):
        S = self.S
        tiles = list(reversed(self.key_tiles(B, blk, s)))
        Qz, LP, LS, WT = B["Qz"], B["LP"], B["LS"], B["WT"]
        G = B["G"]
        qb = 0 if s is None else s * 128
        nq = 512 if s is None else 128
        nt = len(tiles)

        def rng(td):
            trel = td["trel"]
            q0 = max(0, trel) * 128
            return q0, nq - q0, trel >= 0

        for pair in range(4):
            heads = (2 * pair, 2 * pair + 1)
            obase = 6 if pair % 2 == 0 else 0

            def qk(si, stop):
                td = tiles[si]
                q0, n, diag = rng(td)
                for ln in range(2):
                    hh = heads[ln]
                    bk = 2 + ln * 2 + si % 2
                    self.mm(self.ps[bk][:, q0:q0 + n], td["kT"](pair), Qz[hh][:, qb + q0:qb + q0 + n], True, stop,
                            [td["kname"], "a_qz%d" % hh], ["ps%d" % bk])

            def pe_group(si):
                td = tiles[si]
                q0, n, diag = rng(td)
                qk(si, False)
                for ln in range(2):
                    bk = 2 + ln * 2 + si % 2
                    lp = ln * G + si % G
                    self.mm(self.ps[bk][:, q0:q0 + n], self.negtri[:], LP[lp][:, q0:q0 + n], False, False,
                            ["negtri", "b_lp%d" % lp], ["ps%d" % bk])
                    if si > 0:
                        self.mm(self.ps[bk][:, q0:q0 + n], self.negones[:], LS[ln][:, q0:q0 + n], False, False,
                                ["negones", "b_ls%d" % ln], ["ps%d" % bk])
                    if diag:
                        self.mm(self.ps[bk][:, q0:q0 + n], self.ident[:], self.mkneg[:, 0:n], False, True,
                                ["ident", "mkneg"], ["ps%d" % bk])

            for ln in range(2):
                self.memset("dve", LS[ln][:], 0.0, ["b_ls%d" % ln])
            for g0 in range(0, nt, G):
                g1 = min(nt, g0 + G)
                qk(g0, True)
                for si in range(g0, g1):
                    if si + 1 < g1:
                        qk(si + 1, True)
                    td = tiles[si]
                    q0, n, diag = rng(td)
                    for ln in range(2):
                        bk = 2 + ln * 2 + si % 2
                        lp = ln * G + si % G
                        self.act(LP[lp][:, q0:q0 + n], self.ps[bk][:, q0:q0 + n], AF.Softplus, ["ps%d" % bk],
                                 ["b_lp%d" % lp])
                        if diag:
                            self.tt("dve", LP[lp][:, q0:q0 + n], LP[lp][:, q0:q0 + n], self.mk01[:, 0:n],
                                    ALU.mult, ["b_lp%d" % lp, "mk01"], ["b_lp%d" % lp])
                pe_group(g0)
                for si in range(g0, g1):
                    td = tiles[si]
                    q0, n, diag = rng(td)
                    last = (si == nt - 1)
                    if not last:
                        for ln in range(2):
                            lp = ln * G + si % G
                            self.tt("dve", LS[ln][:, q0:q0 + n], LS[ln][:, q0:q0 + n], LP[lp][:, q0:q0 + n], ALU.add,
                                    ["b_ls%d" % ln, "b_lp%d" % lp], ["b_ls%d" % ln])
                    if si + 1 < g1:
                        pe_group(si + 1)
                    for ln in range(2):
                        bk = 2 + ln * 2 + si % 2
                        wt = ln * 2 + si % 2
                        self.act(WT[wt][:, q0:q0 + n], self.ps[bk][:, q0:q0 + n], AF.Exp, ["ps%d" % bk],
                                 ["b_wt%d" % wt])
                    for ln in range(2):
                        wt = ln * 2 + si % 2
                        ob = obase + ln
                        if td["new"]:
                            vap, vn = B["Vn"][:, td["vi"], pair * 128:(pair + 1) * 128], "a_vn.%d" % td["vi"]
                        else:
                            vap, vn = B["V"][:, td["vi"], pair * 128:(pair + 1) * 128], "a_v.%d" % td["vi"]
                        self.mm(self.ps[ob][:, q0:q0 + n], vap, WT[wt][:, q0:q0 + n], si == 0, last,
                                ["b_wt%d" % wt, vn], ["ps%d" % ob])
            self.act(oT[0:64, pair, qb:qb + nq], self.ps[obase][0:64, 0:nq], AF.Copy, ["ps%d" % obase], [oname])
            self.cp("dve", oT[64:128, pair, qb:qb + nq], self.ps[obase + 1][64:128, 0:nq], ["ps%d" % (obase + 1)],
                    [oname])

    def attn_C(self, B, blk, s, oT, oname):
        S = self.S
        l = B["l"]
        tiles = self.key_tiles(B, blk, s)
        Qz, PT, PA, cw = B["Qz"], B["PT"], B["PA"], B["cw"]
        qb = 0 if s is None else s * 128
        nq = 512 if s is None else 128
        pending = [None]
        for hd in range(4):
            def rng(td):
                trel = td["trel"]
                u0 = max(0, trel)
                return u0, nq - u0 * 128, trel >= -1

            def emit_qk(si):
                td = tiles[si]
                u0, n, near = rng(td)
                q0 = u0 * 128
                c0 = (u0 - td["trel"]) * 128
                for m in range(2):
                    bk = m * 2 + si % 2
                    self.mm(self.ps[bk][:, q0:q0 + n], td["kT"](hd), Qz[2 * hd + m][:, qb + q0:qb + q0 + n], True,
                            not near, [td["kname"], "a_qz%d" % (2 * hd + m)], ["ps%d" % bk])
                    if near:
                        self.mm(self.ps[bk][:, q0:q0 + n], self.ident[:], self.tzc[:, hd, c0:c0 + n],
                                False, True, ["ident", "tzc"], ["ps%d" % bk])

            def emit_pv(si):
                td = tiles[si]
                u0, n, near = rng(td)
                q0 = u0 * 128
                last = si == len(tiles) - 1
                for m in range(2):
                    pt = m * 2 + si % 2
                    if td["new"]:
                        vap, vn = B["Vn"][:, td["vi"], hd * 128:(hd + 1) * 128], "a_vn.%d" % td["vi"]
                    else:
                        vap, vn = B["V"][:, td["vi"], hd * 128:(hd + 1) * 128], "a_v.%d" % td["vi"]
                    self.mm(self.ps[4 + m][:, q0:q0 + n], vap, PT[pt][:, q0:q0 + n], si == 0, last,
                            ["a_pt%d" % pt, vn], ["ps%d" % (4 + m)])
                    if m == 0:
                        self.tt("dve", PA[m][:, q0:q0 + n], PA[m][:, q0:q0 + n], PT[pt][:, q0:q0 + n], ALU.add,
                                ["c_pa%d" % m, "a_pt%d" % pt], ["c_pa%d" % m])
                    else:
                        self.mm(self.ps[7][:, q0:q0 + n], self.posones[:], PT[pt][:, q0:q0 + n], si == 0, last,
                                ["a_pt%d" % pt, "posones"], ["ps7"])

            self.memset("dve", PA[0][:, 0:nq], 0.0, ["c_pa0"])
            emit_qk(0)
            for si in range(len(tiles)):
                if si + 1 < len(tiles):
                    emit_qk(si + 1)
                td = tiles[si]
                u0, n, near = rng(td)
                q0 = u0 * 128
                last = si == len(tiles) - 1
                for m in range(2):
                    bk = m * 2 + si % 2
                    pt = m * 2 + si % 2
                    if near:
                        self.act(PT[pt][:, q0:q0 + n], self.ps[bk][:, q0:q0 + n], AF.Exp, ["ps%d" % bk],
                                 ["a_pt%d" % pt])
                    else:
                        self.act(PT[pt][:, q0:q0 + n], self.ps[bk][:, q0:q0 + n], AF.Exp, ["ps%d" % bk, "c15"],
                                 ["a_pt%d" % pt], bias=self.c15[:, hd:hd + 1])
                if si > 0:
                    emit_pv(si - 1)
                if si == 2 and pending[0] is not None:
                    pending[0]()
                    pending[0] = None
            emit_pv(len(tiles) - 1)
            PAb, R, O32, SQ, OS = cw["pab"], cw["r"], cw["o32"], cw["sq"], cw["os"]
            self.cp("act", OS[0][:, 0:nq], self.ps[4][:, 0:nq], ["ps4"], ["c_os0"])
            self.cp("dve", OS[1][:, 0:nq], self.ps[5][:, 0:nq], ["ps5"], ["c_os1"])
            self.act(R[1][:, 0:nq], self.ps[7][:, 0:nq], AF.Ln, ["ps7"], ["c_r1"])
            self.cp("act", PAb[0][:, 0:nq], PA[0][:, 0:nq], ["c_pa0"], ["c_pab0"])
            self.mm(self.ps[6][:, 0:nq], self.posones[:], PAb[0][:, 0:nq], True, True, ["posones", "c_pab0"], ["ps6"])

            def part2(hd=hd):
                self.act(R[1][:, 0:nq], R[1][:, 0:nq], AF.Exp, ["c_r1"], ["c_r1"], scale=-1.0)
                self.act(R[0][:, 0:nq], self.ps[6][:, 0:nq], AF.Ln, ["ps6"], ["c_r0"])
                self.act(R[0][:, 0:nq], R[0][:, 0:nq], AF.Exp, ["c_r0"], ["c_r0"], scale=-1.0)
                self.tt("dve", R[1][:, 0:nq], R[1][:, 0:nq], OS[1][:, 0:nq], ALU.mult, ["c_r1", "c_os1"], ["c_r1"])
                self.tt("dve", R[0][:, 0:nq], R[0][:, 0:nq], OS[0][:, 0:nq], ALU.mult, ["c_r0", "c_os0"], ["c_r0"])
                self.stt("dve", O32[:, 0:nq], R[1][:, 0:nq], self.neglam[:, l:l + 1], R[0][:, 0:nq], ALU.mult,
                         ALU.add, ["c_r0", "c_r1", "neglam"], ["c_o32"])
                self.act(SQ[:, 0:nq], O32[:, 0:nq], AF.Square, ["c_o32"], ["c_sq"])
                self.mm(self.ps[6][:, 0:nq], self.posones[:], SQ[:, 0:nq], True, True, ["posones", "c_sq"], ["ps6"])
                self.ts("dve", R[0][:, 0:nq], self.ps[6][:, 0:nq], 1.0 / 128, EPS, ALU.mult, ALU.add, ["ps6"],
                        ["c_r0"])
                self.act(R[0][:, 0:nq], R[0][:, 0:nq], AF.Ln, ["c_r0"], ["c_r0"])
                self.act(R[0][:, 0:nq], R[0][:, 0:nq], AF.Exp, ["c_r0"], ["c_r0"], scale=-0.5)
                self.stt("dve", oT[:, hd, qb:qb + nq], O32[:, 0:nq], self.sublnT[:, l:l + 1], R[0][:, 0:nq], ALU.mult,
                         ALU.mult, ["c_o32", "sublnT", "c_r0"], [oname])

            pending[0] = part2
        if pending[0] is not None:
            pending[0]()
            pending[0] = None

    def phase_M(self, l):
        S = self.S
        I = self.I
        with ExitStack() as ph:
            wG = self.sb(ph, "m_wg", [128, 8, 3072], BF16)
            wB = self.sb(ph, "m_wb", [128, 12, D], BF16)
            wO = self.sb(ph, "m_wo", [128, 8, D], BF16)
            self.load_w(wG, I["w_in"][l][:, 4608:7680], "m_wg", 4, "m_wg")
            self.load_w(wB, I["w_branch"][l].rearrange("n r c -> (n r) c"), "m_wb", 4, "m_wb")
            self.load_w(wO, I["w_out"][l], "m_wo", 4, "m_wo")
            xnT = [self.sb(ph, "m_xnT%d" % i, [128, 8, BT], BF16) for i in range(2)]
            oT = [[self.sb(ph, "m_oT%d_%d" % (n, i), [128, 4, BT], BF16) for i in range(2)] for n in range(3)]
            gs = [self.sb(ph, "m_g%d" % i, [128, BT], F32) for i in range(3)]
            tm = [self.sb(ph, "m_t%d" % i, [128, BT], F32) for i in range(3)]
            hacc = self.sb(ph, "m_hacc", [128, BT], F32)
            hT = self.sb(ph, "m_hT", [128, 8, BT], BF16)
            xt = [self.sb(ph, "m_xt%d" % i, [128, D], F32) for i in range(2)]

            def loads(blk):
                s_ = blk % 2
                self.dma("sp", xnT[s_][:], self.xnT_d[blk], ["xnT_d.%d" % blk], ["m_xnT%d" % s_], "m_x%d" % s_)
                for n in range(3):
                    self.dma("sp", oT[n][s_][:], self.oT_d[n][blk], ["oT_d%d.%d" % (n, blk)],
                             ["m_oT%d_%d" % (n, s_)], "m_o%d_%d" % (n, s_))

            loads(0)
            xk = 0
            for blk in range(NBLK):
                s_ = blk % 2
                if blk + 1 < NBLK:
                    loads(blk + 1)
                for dc in range(8):
                    for n in range(3):
                        gb = n
                        bb = 3 + n
                        for kc in range(8):
                            self.mm(self.ps[gb][:], wG[:, kc, n * D + dc * 128:n * D + (dc + 1) * 128],
                                    xnT[s_][:, kc, :], kc == 0, kc == 7, ["m_wg.%d" % kc, "m_xnT%d" % s_],
                                    ["ps%d" % gb])
                        self.act(gs[n][:], self.ps[gb][:], AF.Sigmoid, ["ps%d" % gb, "bgate"], ["m_g%d" % n],
                                 bias=self.bgate[:, l, n, dc:dc + 1])
                        for kc in range(4):
                            self.mm(self.ps[bb][:], wB[:, n * 4 + kc, dc * 128:(dc + 1) * 128], oT[n][s_][:, kc, :],
                                    kc == 0, kc == 3, ["m_wb.%d" % (n * 4 + kc), "m_oT%d_%d" % (n, s_)],
                                    ["ps%d" % bb])
                        self.tt("dve", tm[n][:], gs[n][:], self.ps[bb][:], ALU.mult, ["m_g%d" % n, "ps%d" % bb],
                                ["m_t%d" % n])
                    self.tt("dve", hacc[:], tm[0][:], tm[1][:], ALU.add, ["m_t0", "m_t1"], ["m_hacc"])
                    self.tt("dve", hT[:, dc, :], hacc[:], tm[2][:], ALU.add, ["m_hacc", "m_t2"], ["m_hT"])
                for ti in range(4):
                    xs_ = xk % 2
                    xk += 1
                    self.dma("sp", xt[xs_][:], self.x_tile_src(l == 0, blk, ti), ["xres.%d.%d" % (blk, ti)],
                             ["m_xt%d" % xs_], "m_xl%d" % xs_)
                    for half in range(2):
                        ob = (6 if ti % 2 == 0 else 0) + half
                        for kc in range(8):
                            self.mm(self.ps[ob][:], hT[:, kc, ti * 128:(ti + 1) * 128],
                                    wO[:, kc, half * 512:(half + 1) * 512], kc == 0, kc == 7,
                                    ["m_hT", "m_wo.%d" % kc], ["ps%d" % ob])
                        self.tt("dve", xt[xs_][:, half * 512:(half + 1) * 512], xt[xs_][:, half * 512:(half + 1) * 512],
                                self.ps[ob][:], ALU.add, ["m_xt%d" % xs_, "ps%d" % ob], ["m_xt%d" % xs_])
                    r0 = blk * BT + ti * 128
                    self.dma("sp", self.xres[r0:r0 + 128, :], xt[xs_][:], ["m_xt%d" % xs_],
                             ["xres.%d.%d" % (blk, ti)], "m_xs%d" % xs_)
            S.barrier()
            S.replay()

    def phase_F(self, l, wts=None):
        S = self.S
        I, O = self.I, self.O
        with ExitStack() as ph:
            if wts is None:
                wts = self.pre_F(ph, l)
            wU, wD = wts
            xnT = [self.sb(ph, "f_xnT%d" % i, [128, 8, BT], BF16) for i in range(2)]
            gsb = [self.sb(ph, "f_g%d" % i, [128, BT + 2], F32) for i in range(2)]
            cc = [self.sb(ph, "f_c%d" % i, [128, BT], F32) for i in range(2)]
            sg = [self.sb(ph, "f_s%d" % i, [128, BT], F32) for i in range(2)]
            usb = [self.sb(ph, "f_u%d" % i, [128, BT], F32) for i in range(2)]
            halo = self.sb(ph, "f_halo", [128, NFC, 2], F32)
            cvo = self.sb(ph, "f_cvo", [128, NFC, 4, 2], F32)
            hf = self.sb(ph, "f_hf", [128, NFC, BT], BF16)
            xt = [self.sb(ph, "f_xt%d" % i, [128, D], F32) for i in range(2)]
            self.memset("pool", halo[:], 0.0, ["f_halo"])
            self.dma("sp", xnT[0][:], self.xnT_d[0], ["xnT_d.0"], ["f_xnT0"], "f_x0")
            xk = 0
            for blk in range(NBLK):
                s_ = blk % 2
                sample = blk == 8
                if blk + 1 < NBLK:
                    self.dma("sp", xnT[1 - s_][:], self.xnT_d[blk + 1], ["xnT_d.%d" % (blk + 1)],
                             ["f_xnT%d" % (1 - s_)], "f_x%d" % (1 - s_))
                def s0(fc):
                    k2 = fc % 2
                    gbk, ubk = k2, 2 + k2
                    for kc in range(8):
                        self.mm(self.ps[gbk][:], wU[:, kc, fc * 128:(fc + 1) * 128], xnT[s_][:, kc, :], kc == 0,
                                kc == 7, ["f_wu.%d" % kc, "f_xnT%d" % s_], ["ps%d" % gbk])
                    for kc in range(8):
                        self.mm(self.ps[ubk][:], wU[:, kc, DFF + fc * 128:DFF + (fc + 1) * 128], xnT[s_][:, kc, :],
                                kc == 0, kc == 7, ["f_wu.%d" % kc, "f_xnT%d" % s_], ["ps%d" % ubk])

                def s1(fc):
                    k2 = fc % 2
                    gbk, ubk = k2, 2 + k2
                    gname = "f_g%d" % k2
                    G = gsb[k2]
                    self.act(G[:, 2:BT + 2], self.ps[gbk][:], AF.Copy, ["ps%d" % gbk], [gname])
                    self.act(usb[k2][:], self.ps[ubk][:], AF.Copy, ["ps%d" % ubk], ["f_u%d" % k2])
                    if sample:
                        self.cp("pool", G[:, 0:BT].rearrange("p (s c) -> p s c", c=128)[:, :, 0:2],
                                self.cst[:, l, fc, :, :], ["cst", gname], [gname])
                    else:
                        self.cp("pool", G[:, 0:2], halo[:, fc, :], ["f_halo", gname], [gname])
                        if blk < 7:
                            self.cp("pool", halo[:, fc, :], G[:, BT:BT + 2], [gname], ["f_halo"])
                        elif blk == 7:
                            self.cp("pool", cvo[:, fc, 0, :], G[:, BT:BT + 2], [gname], ["f_cvo"])
                    if sample:
                        self.cp("pool", cvo[:, fc, :, :],
                                G[:, 2:BT + 2].rearrange("p (s c) -> p s c", c=128)[:, :, 62:64], [gname], ["f_cvo"])

                def s2(fc):
                    k2 = fc % 2
                    gname, cn = "f_g%d" % k2, "f_c%d" % k2
                    G, C_ = gsb[k2], cc[k2]
                    self.ts("dve", C_[:], G[:, 2:BT + 2], self.convw[:, l, 2, fc:fc + 1],
                            self.convb[:, l, fc:fc + 1], ALU.mult, ALU.add, [gname, "convw", "convb"], [cn])
                    self.stt("dve", C_[:], G[:, 1:BT + 1], self.convw[:, l, 1, fc:fc + 1], C_[:], ALU.mult, ALU.add,
                             [gname, "convw", cn], [cn])
                    self.stt("dve", C_[:], G[:, 0:BT], self.convw[:, l, 0, fc:fc + 1], C_[:], ALU.mult, ALU.add,
                             [gname, "convw", cn], [cn])

                def s3(fc):
                    k2 = fc % 2
                    self.act(sg[k2][:], cc[k2][:], AF.Gelu_apprx_tanh, ["f_c%d" % k2], ["f_s%d" % k2])

                def s4(fc):
                    k2 = fc % 2
                    self.tt("dve", hf[:, fc, :], sg[k2][:], usb[k2][:], ALU.mult, ["f_s%d" % k2, "f_u%d" % k2],
                            ["f_hf"])

                for k in range(NFC + 2):
                    if k < NFC:
                        s0(k)
                    if 0 <= k - 2 < NFC:
                        s3(k - 2)
                    if 0 <= k - 1 < NFC:
                        s1(k - 1)
                    if 0 <= k - 2 < NFC:
                        s4(k - 2)
                    if 0 <= k - 1 < NFC:
                        s2(k - 1)
                if blk == 7:
                    for t_ in range(2):
                        self.dma("pool", O["pconv"][l, t_].rearrange("(c p) -> p c", p=128), cvo[:, :, 0, t_],
                                 ["f_cvo"], [], "f_co%d" % t_, slow=True)
                if sample:
                    for s in range(4):
                        for t_ in range(2):
                            self.dma("pool", O["sconv"][l, s, t_].rearrange("(c p) -> p c", p=128), cvo[:, :, s, t_],
                                     ["f_cvo"], [], "f_co%d" % (s * 2 + t_), slow=True)
                for ti in range(4):
                    xs_ = xk % 2
                    xk += 1
                    r0 = blk * BT + ti * 128
                    self.dma("sp", xt[xs_][:], self.xres[r0:r0 + 128, :], ["xres.%d.%d" % (blk, ti)],
                             ["f_xt%d" % xs_], "f_xl%d" % xs_)
                    for half in range(2):
                        ob = (4 if ti % 2 == 0 else 6) + half
                        for fc in range(NFC):
                            self.mm(self.ps[ob][:], hf[:, fc, ti * 128:(ti + 1) * 128],
                                    wD[:, fc, half * 512:(half + 1) * 512], fc == 0, fc == NFC - 1,
                                    ["f_hf", "f_wd.%d" % fc], ["ps%d" % ob])
                        self.tt("dve", xt[xs_][:, half * 512:(half + 1) * 512], xt[xs_][:, half * 512:(half + 1) * 512],
                                self.ps[ob][:], ALU.add, ["f_xt%d" % xs_, "ps%d" % ob], ["f_xt%d" % xs_])
                    self.dma("sp", self.xres[r0:r0 + 128, :], xt[xs_][:], ["f_xt%d" % xs_],
                             ["xres.%d.%d" % (blk, ti)], "f_xs%d" % xs_)
            S.barrier()
            S.replay()


def _t5_bucket(rel):
    half, max_exact, max_dist = 16, 8, 128
    ret = np.where(rel > 0, half, 0)
    n = np.abs(rel)
    nf = np.maximum(n, 1).astype(np.float32)
    large = max_exact + (np.log(nf / np.float32(max_exact)) / np.float32(np.log(max_dist / max_exact))
                         * np.float32(half - max_exact)).astype(np.int32)
    large = np.minimum(large, half - 1)
    return ret + np.where(n < max_exact, n, large)


def _consts(a_rel_bias, t5_bias):
    k = np.arange(128)[:, None]
    m = np.arange(640)[None, :]
    b_, qq = m // 128, m % 128
    rel = b_ * 128 + qq - k
    idx = np.clip(rel, -128, 128) + 128
    chunk = 2 * b_ + qq // 64 - k // 64
    valid = (chunk >= 0) & (chunk <= 8)
    tza = np.empty((2, 128, 8, 640), np.float32)
    for l in range(2):
        for h in range(8):
            tza[l, :, h, :] = np.where(valid, a_rel_bias[l][idx, h], np.float32(NEG))
    relc = k - qq - b_ * 128
    bucket = _t5_bucket(relc)
    maskc = (k // 64 - qq // 64) > 2 * b_
    tzc = np.empty((128, 4, 640), np.float32)
    for h in range(4):
        tzc[:, h, :] = np.where(maskc, np.float32(NEG), t5_bias[bucket, h])
    c15 = np.broadcast_to(t5_bias[15][None, :], (128, 4)).astype(np.float32).copy()
    q = np.arange(512)[None, :]
    mk01 = (k < q).astype(np.float32)
    mkneg = np.where(k < q, np.float32(0.0), np.float32(NEG)).astype(np.float32)
    return tza, tzc, c15, mk01, mkneg


_NC_CACHE = {}
_DEBUG = {}
PCORES = [0, 1, 4, 5]


def kernel(x_prompt, x_sample, cache_a_k, cache_a_v, cache_b_k, cache_b_v, cache_c_k, cache_c_v,
           state_ffn_conv, norm_mix, w_in, b_gate, a_rel_bias, t5_bias, c_lambda, c_subln,
           w_branch, w_out, norm_ffn, w_up, conv_w, conv_b, w_down, norm_final):
    f = lambda a: np.ascontiguousarray(np.asarray(a, dtype=np.float32))
    x_prompt, x_sample = f(x_prompt), f(x_sample)
    w_in, w_branch, w_out, w_up, w_down = f(w_in), f(w_branch), f(w_out), f(w_up), f(w_down)
    a_rel_bias, t5_bias = f(a_rel_bias), f(t5_bias)
    tza, tzc, c15, mk01, mkneg = _consts(a_rel_bias, t5_bias)
    gains = np.stack([f(norm_mix)[0], f(norm_ffn)[0], f(norm_mix)[1], f(norm_ffn)[1], f(norm_final)], 0)
    gains = np.ascontiguousarray(np.broadcast_to(gains[:, None, :], (5, 128, D)))
    bgate = np.ascontiguousarray(f(b_gate).reshape(2, 3, 8, 128).transpose(3, 0, 1, 2))
    convw = np.ascontiguousarray(f(conv_w).reshape(2, 3, NFC, 128).transpose(3, 0, 1, 2))
    convb = np.ascontiguousarray(f(conv_b).reshape(2, NFC, 128).transpose(2, 0, 1))
    lam = np.ascontiguousarray(np.broadcast_to(f(c_lambda).reshape(1, 2, 256), (128, 2, 256)))
    subln = np.ascontiguousarray(np.broadcast_to(f(c_subln).reshape(1, 2, 128), (128, 2, 128)))
    sublnT = np.ascontiguousarray(f(c_subln).T)
    cak, cav = f(cache_a_k).reshape(2, 32, 512, 512), f(cache_a_v).reshape(2, 32, 512, 512)
    cbk, cbv = f(cache_b_k).reshape(2, 32, 1024, 512), f(cache_b_v).reshape(2, 32, 1024, 512)
    cck, ccv = f(cache_c_k).reshape(2, 32, 1024, 512), f(cache_c_v).reshape(2, 32, 1024, 512)
    cst_all = f(state_ffn_conv)
    shared = dict(w_in=w_in, w_branch=w_branch, w_out=w_out, w_up=w_up, w_down=w_down, gains=gains, bgate=bgate,
                  convw=convw, convb=convb, lam=lam, subln=subln, sublnT=sublnT, tza=tza, tzc=tzc, c15=c15, mk01=mk01, mkneg=mkneg)
    in_maps = []
    zero_xp = np.zeros((SEQ, D), np.float32)
    for c in range(8):
        sl = slice(4 * c, 4 * c + 4)
        xs = np.zeros((4, 128, D), np.float32)
        xs[:, 0:64, :] = x_sample[sl]
        cst = cst_all[:, sl].reshape(2, 4, 2, NFC, 128).transpose(4, 0, 3, 1, 2)
        m = dict(shared)
        xp_c = x_prompt[PCORES.index(c)] if c in PCORES else zero_xp
        m.update(xp=xp_c, xs=xs.reshape(BT, D),
                 cak=np.ascontiguousarray(cak[:, sl]), cav=np.ascontiguousarray(cav[:, sl]),
                 cbk=np.ascontiguousarray(cbk[:, sl]), cbv=np.ascontiguousarray(cbv[:, sl]),
                 cck=np.ascontiguousarray(cck[:, sl]), ccv=np.ascontiguousarray(ccv[:, sl]),
                 cst=np.ascontiguousarray(cst))
        in_maps.append(m)
    if _DEBUG.get("prep_only"):
        return in_maps
    if "nc" not in _NC_CACHE:
        _NC_CACHE["nc"] = Builder().build()
    res = run_bass_kernel_spmd(_NC_CACHE["nc"], in_maps, core_ids=list(range(8)))
    R = res.results
    y_prompt = np.stack([R[b]["yp"] for b in PCORES], 0)
    y_sample = np.concatenate([R[c]["ys"] for c in range(8)], 0)

    def pst(name, shape):
        return np.stack([R[b][name] for b in PCORES], 1).reshape(shape)

    def sst(name, shape):
        return np.concatenate([R[c][name] for c in range(8)], 1).reshape(shape)

    outs = (
        y_prompt, y_sample,
        pst("pak", (2, 4, 512, 8, 64)), pst("pav", (2, 4, 512, 8, 64)),
        pst("pbk", (2, 4, SEQ, 8, 64)), pst("pbv", (2, 4, SEQ, 8, 64)),
        pst("pck", (2, 4, SEQ, 4, 2, 64)), pst("pcv", (2, 4, SEQ, 4, 128)),
        pst("pconv", (2, 4, 2, DFF)),
        sst("sak", (2, 32, 512, 8, 64)), sst("sav", (2, 32, 512, 8, 64)),
        sst("sbk", (2, 32, 64, 8, 64)), sst("sbv", (2, 32, 64, 8, 64)),
        sst("sck", (2, 32, 64, 4, 2, 64)), sst("scv", (2, 32, 64, 4, 128)),
        sst("sconv", (2, 32, 2, DFF)),
    )
    return tuple(np.ascontiguousarray(o, dtype=np.float32) for o in outs)
```

```python
import numpy as np
from contextlib import ExitStack

import concourse.bass as bass
import concourse.mybir as mybir
from concourse.bass_utils import run_bass_kernel_spmd

F32 = mybir.dt.float32
BF16 = mybir.dt.bfloat16
AF = mybir.ActivationFunctionType
ALU = mybir.AluOpType
AX = mybir.AxisListType

D = 1024
NBLK = 9
BT = 512
SEQ = 4096
DFF = 2816
NFC = 22
EPS = 1e-6
NEG = -30000.0
GELU_K = 1.5957691216057308


class Sched:
    ENG = ("pe", "act", "dve", "pool", "sp")

    def __init__(self, nc, es):
        self.nc = nc
        self.es = es
        self.ops = {e: [] for e in self.ENG}
        self.sem = {e: es.enter_context(nc.semaphore("s_" + e)) for e in self.ENG}
        self.cnt = {e: 0 for e in self.ENG}
        self.known = {e: {} for e in self.ENG}
        self.res_w = {}
        self.res_r = {}
        self.dma = {}
        self.nwait = 0

    def _wait(self, eng, tok):
        if tok is None:
            return
        key, h, v = tok
        if key == "pe" and eng == "pe":
            return
        if self.known[eng].get(key, 0) >= v:
            return
        self.known[eng][key] = v
        self.ops[eng].append(("wait", h, v))
        self.nwait += 1

    def op(self, eng, fn, reads=(), writes=(), dma=None):
        writes = list(writes) + [r for r in reads if r.startswith("ps")]
        reads = [r for r in reads if not r.startswith("ps")]
        if dma is not None:
            if dma not in self.dma:
                self.dma[dma] = [self.es.enter_context(self.nc.semaphore("d_" + dma.replace(".", "_"))), 0]
            writes.append("__dma." + dma)
        for r in reads:
            self._wait(eng, self.res_w.get(r))
        for w in writes:
            self._wait(eng, self.res_w.get(w))
            rr = self.res_r.get(w)
            if rr:
                for t in rr.values():
                    self._wait(eng, t)
        if dma is None:
            self.cnt[eng] += 1
            tok = (eng, self.sem[eng], self.cnt[eng])
            self.ops[eng].append(("op", fn, self.sem[eng], 1))
        else:
            d = self.dma[dma]
            d[1] += 16
            tok = (dma, d[0], d[1])
            self.ops[eng].append(("op", fn, d[0], 16))
        for r in reads:
            self.res_r.setdefault(r, {})[tok[0]] = tok
        for w in writes:
            self.res_w[w] = tok
            self.res_r[w] = {}
        return tok

    def barrier(self):
        toks = [(e, self.sem[e], self.cnt[e]) for e in self.ENG if self.cnt[e] > 0]
        toks += [(k, d[0], d[1]) for k, d in self.dma.items() if d[1] > 0]
        for e in self.ENG:
            for t in toks:
                if t[0] != e:
                    self._wait(e, t)
        self.res_w = {}
        self.res_r = {}

    def replay(self):
        with self.nc.Block() as block:
            self._replay(block)
        self.ops = {e: [] for e in self.ENG}

    def _replay(self, block):
        def run(e, handle):
            for it in self.ops[e]:
                if it[0] == "wait":
                    handle.wait_ge(it[1], it[2])
                else:
                    it[1](handle).then_inc(it[2], it[3])

        @block.tensor
        def _(h):
            run("pe", h)

        @block.scalar
        def _(h):
            run("act", h)

        @block.vector
        def _(h):
            run("dve", h)

        @block.gpsimd
        def _(h):
            run("pool", h)

        @block.sync
        def _(h):
            run("sp", h)


class Builder:
    def __init__(self, nphase=None):
        self.nc = bass.Bass("TRN2", target_bir_lowering=False)
        self.es = ExitStack()
        self.evac_rr = 0
        self.nphase = nphase

    def din(self, name, shape, dt=F32):
        return self.nc.dram_tensor(name, list(shape), dt, kind="ExternalInput").ap()

    def dout(self, name, shape, dt=F32):
        return self.nc.dram_tensor(name, list(shape), dt, kind="ExternalOutput").ap()

    def dscr(self, name, shape, dt):
        return self.nc.dram_tensor(name, list(shape), dt).ap()

    def sb(self, st, name, shape, dt):
        self.uid = getattr(self, "uid", 0) + 1
        return st.enter_context(self.nc.sbuf_tensor("%s_u%d" % (name, self.uid), list(shape), dt))

    def mm(self, out, lhsT, rhs, start, stop, reads, writes):
        self.S.op("pe", lambda h: h.matmul(out, lhsT=lhsT, rhs=rhs, start=start, stop=stop,
                                           skip_group_check=True), reads=reads, writes=writes)

    def act(self, out, in_, func, reads, writes, bias=None, scale=None, accum_out=None):
        kw = {}
        if bias is not None:
            kw["bias"] = bias
        if scale is not None:
            kw["scale"] = scale
        if accum_out is not None:
            kw["accum_out"] = accum_out
        self.S.op("act", lambda h: h.activation(out=out, in_=in_, func=func, **kw), reads=reads, writes=writes)

    def ts(self, eng, out, in0, s1, s2, op0, op1, reads, writes):
        if s2 is None:
            self.S.op(eng, lambda h: h.tensor_scalar(out=out, in0=in0, scalar1=s1, scalar2=None, op0=op0),
                      reads=reads, writes=writes)
        else:
            self.S.op(eng, lambda h: h.tensor_scalar(out=out, in0=in0, scalar1=s1, scalar2=s2, op0=op0, op1=op1),
                      reads=reads, writes=writes)

    def stt(self, eng, out, in0, scalar, in1, op0, op1, reads, writes):
        self.S.op(eng, lambda h: h.scalar_tensor_tensor(out=out, in0=in0, scalar=scalar, in1=in1, op0=op0, op1=op1),
                  reads=reads, writes=writes)

    def tt(self, eng, out, in0, in1, op, reads, writes):
        self.S.op(eng, lambda h: h.tensor_tensor(out=out, in0=in0, in1=in1, op=op), reads=reads, writes=writes)

    def cp(self, eng, out, in_, reads, writes):
        if eng == "act":
            self.S.op("act", lambda h: h.activation(out=out, in_=in_, func=AF.Copy), reads=reads, writes=writes)
        else:
            self.S.op(eng, lambda h: h.tensor_copy(out=out, in_=in_), reads=reads, writes=writes)

    def memset(self, eng, ap, val, writes):
        self.S.op(eng, lambda h: h.memset(ap, val), writes=writes)

    def dma(self, eng, out, in_, reads, writes, sem, slow=False):
        if slow:
            self.S.op(eng, lambda h: h.dma_start(out=out, in_=in_, allow_slow_non_contiguous=True), reads=reads,
                      writes=writes, dma=sem)
        else:
            self.S.op(eng, lambda h: h.dma_start(out=out, in_=in_), reads=reads, writes=writes, dma=sem)

    def evac_eng(self):
        self.evac_rr += 1
        return "act" if self.evac_rr % 2 else "dve"

    def declare(self):
        nc = self.nc
        I = {}
        I["xp"] = self.din("xp", [SEQ, D])
        I["xs"] = self.din("xs", [BT, D])
        I["cak"] = self.din("cak", [2, 4, 512, 512])
        I["cav"] = self.din("cav", [2, 4, 512, 512])
        I["cbk"] = self.din("cbk", [2, 4, 1024, 512])
        I["cbv"] = self.din("cbv", [2, 4, 1024, 512])
        I["cck"] = self.din("cck", [2, 4, 1024, 512])
        I["ccv"] = self.din("ccv", [2, 4, 1024, 512])
        I["cst"] = self.din("cst", [128, 2, NFC, 4, 2])
        I["w_in"] = self.din("w_in", [2, D, 7680])
        I["w_branch"] = self.din("w_branch", [2, 3, 512, D])
        I["w_out"] = self.din("w_out", [2, D, D])
        I["w_up"] = self.din("w_up", [2, D, 2 * DFF])
        I["w_down"] = self.din("w_down", [2, DFF, D])
        I["gains"] = self.din("gains", [5, 128, D])
        I["bgate"] = self.din("bgate", [128, 2, 3, 8])
        I["convw"] = self.din("convw", [128, 2, 3, NFC])
        I["convb"] = self.din("convb", [128, 2, NFC])
        I["lam"] = self.din("lam", [128, 2, 256])
        I["subln"] = self.din("subln", [128, 2, 128])
        I["sublnT"] = self.din("sublnT", [128, 2])
        I["tza"] = self.din("tza", [2, 128, 8, 640])
        I["tzc"] = self.din("tzc", [128, 4, 640])
        I["c15"] = self.din("c15", [128, 4])
        I["mk01"] = self.din("mk01", [128, 512])
        I["mkneg"] = self.din("mkneg", [128, 512])
        self.I = I
        O = {}
        O["yp"] = self.dout("yp", [SEQ, D])
        O["ys"] = self.dout("ys", [4, 64, D])
        O["pak"] = self.dout("pak", [2, 512, 512])
        O["pav"] = self.dout("pav", [2, 512, 512])
        for n in ("pbk", "pbv", "pck", "pcv"):
            O[n] = self.dout(n, [2, SEQ, 512])
        O["pconv"] = self.dout("pconv", [2, 2, DFF])
        O["sak"] = self.dout("sak", [2, 4, 512, 512])
        O["sav"] = self.dout("sav", [2, 4, 512, 512])
        for n in ("sbk", "sbv", "sck", "scv"):
            O[n] = self.dout(n, [2, 4, 64, 512])
        O["sconv"] = self.dout("sconv", [2, 4, 2, DFF])
        self.O = O
        self.xres = self.dscr("xres", [NBLK * BT, D], F32)
        self.xnT_d = self.dscr("xnT_d", [NBLK, 128, 8, BT], BF16)
        self.oT_d = [self.dscr("oT_d%d" % i, [NBLK, 128, 4, BT], BF16) for i in range(3)]

    def build(self):
        nc = self.nc
        es = self.es
        self.declare()
        I = self.I
        self.S = Sched(nc, es)
        S = self.S
        self.ps = [es.enter_context(nc.psum_tensor("psb%d" % i, [128, 512], F32)) for i in range(8)]
        g = es
        self.ident = self.sb(g, "ident", [128, 128], BF16)
        self.negtri = self.sb(g, "negtri", [128, 128], BF16)
        self.negones = self.sb(g, "negones", [128, 128], BF16)
        self.posones = self.sb(g, "posones", [128, 128], BF16)
        self.sublnT = self.sb(g, "sublnT", [128, 2], F32)
        self.bgate = self.sb(g, "bgates", [128, 2, 3, 8], F32)
        self.convw = self.sb(g, "convws", [128, 2, 3, NFC], F32)
        self.convb = self.sb(g, "convbs", [128, 2, NFC], F32)
        self.cst = self.sb(g, "csts", [128, 2, NFC, 4, 2], F32)
        self.lam = self.sb(g, "lams", [128, 2, 256], F32)
        self.subln = self.sb(g, "sublns", [128, 2, 128], F32)
        self.lamw = self.sb(g, "lamw", [128, 2, 2, 64], F32)
        self.lams = self.sb(g, "lamsm", [128, 2, 8], F32)
        self.neglam = self.sb(g, "neglam", [128, 2], F32)
        tmpf = self.sb(g, "ctmpf", [128, 128], F32)

        self.memset("pool", tmpf[:], 0.0, ["tmpf"])
        S.op("pool", lambda h: h.affine_select(out=tmpf[:], in_=tmpf[:], pattern=[[-1, 128]],
                                               compare_op=ALU.not_equal, fill=1.0, base=0, channel_multiplier=1),
             reads=["tmpf"], writes=["tmpf"])
        self.cp("dve", self.ident[:], tmpf[:], ["tmpf"], ["ident"])
        self.memset("pool", tmpf[:], -1.0, ["tmpf"])
        S.op("pool", lambda h: h.affine_select(out=tmpf[:], in_=tmpf[:], pattern=[[-1, 128]],
                                               compare_op=ALU.is_ge, fill=0.0, base=0, channel_multiplier=1),
             reads=["tmpf"], writes=["tmpf"])
        self.cp("dve", self.negtri[:], tmpf[:], ["tmpf"], ["negtri"])
        self.memset("pool", self.negones[:], -1.0, ["negones"])
        self.memset("pool", self.posones[:], 1.0, ["posones"])
        self.dma("sp", self.sublnT[:], I["sublnT"], [], ["sublnT"], "c3")
        self.dma("sp", self.bgate[:], I["bgate"], [], ["bgate"], "c4")
        self.dma("sp", self.convw[:], I["convw"], [], ["convw"], "c5")
        self.dma("sp", self.convb[:], I["convb"], [], ["convb"], "c6")
        self.dma("sp", self.cst[:], I["cst"], [], ["cst"], "c7")
        self.dma("sp", self.lam[:], I["lam"], [], ["lam"], "c8")
        self.dma("sp", self.subln[:], I["subln"], [], ["subln"], "c9")
        for l in range(2):
            lv = self.lam[:, l, :].rearrange("p (a b c) -> p a b c", a=2, b=2, c=64)
            self.tt("dve", self.lamw[:, l, :, :], lv[:, :, 0, :], lv[:, :, 1, :], ALU.mult, ["lam"], ["lamw"])
            S.op("dve", lambda h, l=l: h.reduce_sum(out=self.lams[:, l, 0:2], in_=self.lamw[:, l, :, :], axis=AX.X),
                 reads=["lamw"], writes=["lams"])
            self.act(self.lams[:, l, 2:4], self.lams[:, l, 0:2], AF.Exp, ["lams"], ["lams"])
            self.tt("dve", self.lams[:, l, 4:5], self.lams[:, l, 3:4], self.lams[:, l, 2:3], ALU.subtract,
                    ["lams"], ["lams"])
            lam_init = 0.8 - 0.6 * float(np.exp(-0.3 * l))
            self.ts("dve", self.neglam[:, l:l + 1], self.lams[:, l, 4:5], -lam_init, None, ALU.add, None,
                    ["lams"], ["neglam"])
            self.ts("dve", self.sublnT[:, l:l + 1], self.sublnT[:, l:l + 1], 1.0 - lam_init, None, ALU.mult, None,
                    ["sublnT"], ["sublnT"])
        S.barrier()
        S.replay()

        def layer(l):
            with ExitStack() as c1:
                wB = self.pre_attn(c1, l, "B")
                with ExitStack() as c2:
                    wA = self.pre_attn(c2, l, "A")
                    self.phase_N(l, 0)
                    self.phase_attn(l, "A", wA)
                self.phase_attn(l, "B", wB)
            self.phase_attn(l, "C")
            self.phase_M(l)
            with ExitStack() as c3:
                wts = self.pre_F(c3, l)
                self.phase_N(l, 1)
                self.phase_F(l, wts)

        if self.nphase is None:
            for l in range(2):
                layer(l)
            self.phase_N(2, 2)
        else:
            plist = []
            for l in range(2):
                plist += [lambda l=l: self.phase_N(l, 0), lambda l=l: self.phase_attn(l, "A"),
                          lambda l=l: self.phase_attn(l, "B"), lambda l=l: self.phase_attn(l, "C"),
                          lambda l=l: self.phase_M(l), lambda l=l: self.phase_N(l, 1), lambda l=l: self.phase_F(l)]
            plist.append(lambda: self.phase_N(2, 2))
            for p in plist[:self.nphase]:
                p()
        es.close()
        return nc

    def x_tile_src(self, from_input, blk, ti):
        if from_input:
            if blk < 8:
                return self.I["xp"][blk * BT + ti * 128: blk * BT + (ti + 1) * 128, :]
            return self.I["xs"][ti * 128:(ti + 1) * 128, :]
        r0 = blk * BT + ti * 128
        return self.xres[r0:r0 + 128, :]

    def phase_N(self, l, kind):
        S = self.S
        with ExitStack() as ph:
            gain = self.sb(ph, "n_gain", [128, D], F32)
            NS = 8
            xt = [self.sb(ph, "n_xt%d" % i, [128, D], F32) for i in range(NS)]
            junk = self.sb(ph, "n_junk", [128, D], F32)
            xn = [self.sb(ph, "n_xn%d" % i, [128, D], BF16) for i in range(2)] if kind != 2 else None
            yt = [self.sb(ph, "n_yt%d" % i, [128, D], F32) for i in range(2)] if kind == 2 else None
            st = [self.sb(ph, "n_st%d" % i, [128, 4], F32) for i in range(NS)]
            xnT = [self.sb(ph, "n_xnT%d" % i, [128, 8, BT], BF16) for i in range(2)] if kind != 2 else None
            gi = {0: 2 * l, 1: 2 * l + 1, 2: 4}[kind]
            self.dma("sp", gain[:], self.I["gains"][gi], [], ["n_gain"], "n_g")
            from_input = (kind == 0 and l == 0)
            tiles = [(blk, ti) for blk in range(NBLK) for ti in range(4)]

            def sa(k):
                blk, ti = tiles[k]
                s = k % NS
                self.dma("sp", xt[s][:], self.x_tile_src(from_input, blk, ti), ["xres.%d.%d" % (blk, ti)],
                         ["n_xt%d" % s], "n_x%d" % s)
                self.memset("dve", st[s][:, 0:1], 0.0, ["n_st%d" % s])
                self.act(junk[:], xt[s][:], AF.Square, ["n_xt%d" % s, "n_st%d" % s], ["n_junk", "n_st%d" % s],
                         accum_out=st[s][:, 0:1])
                self.ts("dve", st[s][:, 1:2], st[s][:, 0:1], 1.0 / D, EPS, ALU.mult, ALU.add,
                        ["n_st%d" % s], ["n_st%d" % s])

            def sb_(k):
                blk, ti = tiles[k]
                s = k % NS
                s2 = k % 2
                bs = blk % 2
                self.act(st[s][:, 2:3], st[s][:, 1:2], AF.Ln, ["n_st%d" % s], ["n_st%d" % s])
                self.act(st[s][:, 3:4], st[s][:, 2:3], AF.Exp, ["n_st%d" % s], ["n_st%d" % s], scale=-0.5)
                if kind == 2:
                    self.stt("dve", yt[s2][:], xt[s][:], st[s][:, 3:4], gain[:], ALU.mult, ALU.mult,
                             ["n_xt%d" % s, "n_st%d" % s, "n_gain"], ["n_yt%d" % s2])
                    if blk < 8:
                        self.dma("sp", self.O["yp"][blk * BT + ti * 128: blk * BT + (ti + 1) * 128, :], yt[s2][:],
                                 ["n_yt%d" % s2], [], "n_y%d" % s2)
                    else:
                        self.dma("sp", self.O["ys"][ti], yt[s2][0:64, :], ["n_yt%d" % s2], [], "n_y%d" % s2)
                    return
                self.stt("dve", xn[s2][:], xt[s][:], st[s][:, 3:4], gain[:], ALU.mult, ALU.mult,
                         ["n_xt%d" % s, "n_st%d" % s, "n_gain"], ["n_xn%d" % s2])
                pb = s2
                psT = self.ps[pb][:].bitcast(BF16)
                for kc in range(8):
                    S.op("pe", lambda h, kc=kc, psT=psT, s2=s2: h.transpose(
                        out=psT[:, kc * 128:(kc + 1) * 128], in_=xn[s2][:, kc * 128:(kc + 1) * 128],
                        identity=self.ident[:]), reads=["n_xn%d" % s2, "ident"], writes=["ps%d" % pb])
                self.cp(self.evac_eng(), xnT[bs][:, :, ti * 128:(ti + 1) * 128],
                        psT.rearrange("p (k t) -> p k t", k=8), ["ps%d" % pb], ["n_xnT%d" % bs])
                if ti == 3:
                    self.dma("sp", self.xnT_d[blk], xnT[bs][:], ["n_xnT%d" % bs], ["xnT_d.%d" % blk], "n_o%d" % bs)

            nt = len(tiles)
            LA = 3
            for k in range(LA):
                sa(k)
            for k in range(nt):
                if k + LA < nt:
                    sa(k + LA)
                sb_(k)
            S.barrier()
            S.replay()

    def load_w(self, dst, src, name, nsplit, sem):
        kcn = dst.shape[1]
        for kc in range(kcn):
            self.dma("pool", dst[:, kc, :], src[kc * 128:(kc + 1) * 128, :], [], ["%s.%d" % (name, kc)],
                     "%s%d" % (sem, kc % nsplit))

    def proj_fm(self, w, wname, col0, xnT, xname, bank, nk=8):
        for kc in range(nk):
            self.mm(self.ps[bank][:], w[:, kc, col0:col0 + 128], xnT[:, kc, :], kc == 0, kc == nk - 1,
                    ["%s.%d" % (wname, kc), xname], ["ps%d" % bank])

    def proj_tm(self, w, wname, col0, xnT, xname, ti, bank, nk=8):
        for kc in range(nk):
            self.mm(self.ps[bank][:], xnT[:, kc, ti * 128:(ti + 1) * 128], w[:, kc, col0:col0 + 512], kc == 0,
                    kc == nk - 1, ["%s.%d" % (wname, kc), xname], ["ps%d" % bank])

    def pre_attn(self, st, l, br):
        col0 = "ABC".index(br) * 1536
        w = self.sb(st, "a_w", [128, 8, 1536], BF16)
        self.load_w(w, self.I["w_in"][l][:, col0:col0 + 1536], "a_w", 4, "a_w" + br)
        return w

    def pre_F(self, st, l):
        wU = self.sb(st, "f_wu", [128, 8, 2 * DFF], BF16)
        wD = self.sb(st, "f_wd", [128, NFC, D], BF16)
        self.load_w(wU, self.I["w_up"][l], "f_wu", 4, "f_wu")
        self.load_w(wD, self.I["w_down"][l], "f_wd", 4, "f_wd")
        return wU, wD

    def phase_attn(self, l, br, w=None):
        S = self.S
        I, O = self.I, self.O
        bi = "ABC".index(br)
        col0 = bi * 1536
        with ExitStack() as ph:
            if w is None:
                w = self.pre_attn(ph, l, br)
            xnT = [self.sb(ph, "a_xnT%d" % i, [128, 8, BT], BF16) for i in range(2)]
            Qz = [self.sb(ph, "a_qz%d" % i, [128, BT], BF16) for i in range(8)]
            stg = [self.sb(ph, "a_stg%d" % i, [128, 512], F32) for i in range(4)]
            oT = [self.sb(ph, "a_oT%d" % i, [128, 4, BT], BF16) for i in range(2)]
            kctok = [self.sb(ph, "a_kct%d" % i, [128, 8, 512], BF16) for i in range(2)]
            KTn = self.sb(ph, "a_ktn", [128, 4, BT], BF16)
            for i in range(8):
                self.memset("pool", Qz[i][:], 0.0, ["a_qz%d" % i])
            for i in range(2):
                self.memset("pool", oT[i][:], 0.0, ["a_oT%d" % i])
            B = {}
            if br == "A":
                tza = self.sb(ph, "a_tza", [128, 8, 640], BF16)
                self.dma("pool", tza[:], I["tza"][l], [], ["a_tza"], "a_tz")
                KT = self.sb(ph, "a_kt", [128, 4, 1024], BF16)
                V = self.sb(ph, "a_v", [128, 8, 8, 65], BF16)
                Vn = self.sb(ph, "a_vn", [128, 4, 8, 65], BF16)
                self.memset("pool", V[:], 1.0, ["a_v.%d" % i for i in range(8)])
                self.memset("pool", Vn[:], 1.0, ["a_vn.%d" % i for i in range(4)])
                PT = [self.sb(ph, "a_pt%d" % i, [128, BT], BF16) for i in range(4)]
                otok = self.sb(ph, "a_otok", [128, 4, 512], BF16)
                dd = self.sb(ph, "a_dd", [128, 2, 8], F32)
                B.update(tza=tza, KT=KT, V=V, Vn=Vn, PT=PT, otok=otok, dd=dd)
            elif br == "B":
                KT = self.sb(ph, "a_kt", [128, 4, SEQ], BF16)
                V = self.sb(ph, "a_v", [128, 32, 512], BF16)
                Vn = self.sb(ph, "a_vn", [128, 4, 512], BF16)
                G = 16
                LP = [self.sb(ph, "b_lp%d" % i, [128, BT], BF16) for i in range(2 * G)]
                LS = [self.sb(ph, "b_ls%d" % i, [128, BT], BF16) for i in range(2)]
                WT = [self.sb(ph, "b_wt%d" % i, [128, BT], BF16) for i in range(4)]
                self.mk01 = self.sb(ph, "mk01s", [128, 512], BF16)
                self.mkneg = self.sb(ph, "mknegs", [128, 512], BF16)
                self.dma("pool", self.mk01[:], I["mk01"], [], ["mk01"], "c0")
                self.dma("pool", self.mkneg[:], I["mkneg"], [], ["mkneg"], "c1")
                B.update(KT=KT, V=V, Vn=Vn, G=G, LP=LP, LS=LS, WT=WT)
            else:
                KT = self.sb(ph, "a_kt", [128, 4, SEQ], BF16)
                V = self.sb(ph, "a_v", [128, 32, 512], BF16)
                Vn = self.sb(ph, "a_vn", [128, 4, 512], BF16)
                PT = [self.sb(ph, "a_pt%d" % i, [128, BT], BF16) for i in range(4)]
                PA = [self.sb(ph, "c_pa%d" % i, [128, BT], F32) for i in range(2)]
                cw = dict(pab=[self.sb(ph, "c_pab%d" % i, [128, BT], BF16) for i in range(2)],
                          r=[self.sb(ph, "c_r%d" % i, [128, BT], F32) for i in range(2)],
                          o32=self.sb(ph, "c_o32", [128, BT], F32), sq=self.sb(ph, "c_sq", [128, BT], BF16),
                          os=[self.sb(ph, "c_os%d" % i, [128, BT], F32) for i in range(2)])
                self.tzc = self.sb(ph, "tzcs", [128, 4, 640], BF16)
                self.c15 = self.sb(ph, "c15s", [128, 4], F32)
                self.dma("pool", self.tzc[:], I["tzc"], [], ["tzc"], "c2")
                self.dma("sp", self.c15[:], I["c15"], [], ["c15"], "c3")
                B.update(KT=KT, V=V, Vn=Vn, PT=PT, PA=PA, cw=cw)
            B.update(w=w, xnT=xnT, Qz=Qz, stg=stg, oT=oT, kctok=kctok, KTn=KTn, l=l, br=br)
            self.stg_i = 0
            self.dma("sp", xnT[0][:], self.xnT_d[0], ["xnT_d.0"], ["a_xnT0"], "a_x0")
            for blk in range(NBLK):
                xs_ = blk % 2
                if blk + 1 < NBLK:
                    self.dma("sp", xnT[1 - xs_][:], self.xnT_d[blk + 1], ["xnT_d.%d" % (blk + 1)],
                             ["a_xnT%d" % (1 - xs_)], "a_x%d" % (1 - xs_))
                if "blocks" in _DEBUG and blk not in _DEBUG["blocks"]:
                    continue
                self.attn_block(B, blk)
            S.barrier()
            S.replay()

    def stage_out(self, B, bank, dsts):
        i = self.stg_i % 4
        self.stg_i += 1
        st = B["stg"][i]
        self.cp(self.evac_eng(), st[:], self.ps[bank][:], ["ps%d" % bank], ["a_stg%d" % i])
        for (dst, npart) in dsts:
            self.dma("sp", dst, st[0:npart, :], ["a_stg%d" % i], [], "a_so%d" % i)

    def attn_block(self, B, blk):
        S = self.S
        I, O = self.I, self.O
        l, br = B["l"], B["br"]
        xs_ = blk % 2
        xnT = B["xnT"][xs_]
        xname = "a_xnT%d" % xs_
        w = B["w"]
        Qz = B["Qz"]
        sample = (blk == 8)
        pbank = [0, 1]
        pk = 0
        if sample:
            self.load_cache(B, 0)
            self.load_cache(B, 1)
        for oc in range(4):
            bk = pbank[pk % 2]
            pk += 1
            self.proj_fm(w, "a_w", oc * 128, xnT, xname, bk)
            self.act(Qz[2 * oc][0:64, :], self.ps[bk][0:64, :], AF.Copy, ["ps%d" % bk], ["a_qz%d" % (2 * oc)],
                     scale=0.125)
            self.ts("dve", Qz[2 * oc + 1][64:128, :], self.ps[bk][64:128, :], 0.125, None, ALU.mult, None,
                    ["ps%d" % bk], ["a_qz%d" % (2 * oc + 1)])
        for oc in range(4):
            bk = pbank[pk % 2]
            pk += 1
            self.proj_fm(w, "a_w", 512 + oc * 128, xnT, xname, bk)
            if sample:
                dst, dn = B["KTn"][:, oc, :], "a_ktn"
            elif br == "A":
                dst, dn = B["KT"][:, oc, (blk % 2) * 512:(blk % 2) * 512 + 512], "a_kt.%d" % (blk % 2)
            else:
                dst, dn = B["KT"][:, oc, blk * 512:(blk + 1) * 512], "a_kt.%d" % blk
            self.cp(self.evac_eng(), dst, self.ps[bk][:], ["ps%d" % bk], [dn])
        kname = {"A": ("pak", "sak"), "B": ("pbk", "sbk"), "C": ("pck", "sck")}[br]
        vname = {"A": ("pav", "sav"), "B": ("pbv", "sbv"), "C": ("pcv", "scv")}[br]
        for ti in range(4):
            need_k = sample or br != "A" or blk == 7
            if need_k:
                bk = pbank[pk % 2]
                pk += 1
                self.proj_tm(w, "a_w", 512, xnT, xname, ti, bk)
                if sample:
                    if br == "A":
                        d = [(O[kname[1]][l, ti, 448:512, :], 64)]
                    else:
                        d = [(O[kname[1]][l, ti], 64)]
                elif br == "A":
                    d = [(O[kname[0]][l, ti * 128:(ti + 1) * 128, :], 128)]
                else:
                    r0 = blk * BT + ti * 128
                    d = [(O[kname[0]][l, r0:r0 + 128, :], 128)]
                self.stage_out(B, bk, d)
            bk = pbank[pk % 2]
            pk += 1
            self.proj_tm(w, "a_w", 1024, xnT, xname, ti, bk)
            if br == "A":
                if sample:
                    vd, vn = B["Vn"][:, ti, :, 0:64], "a_vn.%d" % ti
                else:
                    vt = (blk % 2) * 4 + ti
                    vd, vn = B["V"][:, vt, :, 0:64], "a_v.%d" % vt
                src = self.ps[bk][:].rearrange("p (h d) -> p h d", h=8)
            elif br == "B":
                if sample:
                    vd, vn = B["Vn"][:, ti, :], "a_vn.%d" % ti
                else:
                    vd, vn = B["V"][:, blk * 4 + ti, :], "a_v.%d" % (blk * 4 + ti)
                src = self.ps[bk][:]
            else:
                if sample:
                    vd, vn = B["Vn"][:, ti, :], "a_vn.%d" % ti
                else:
                    vd, vn = B["V"][:, blk * 4 + ti, :], "a_v.%d" % (blk * 4 + ti)
                src = self.ps[bk][:]
            self.cp(self.evac_eng(), vd, src, ["ps%d" % bk], [vn])
            need_v = sample or br != "A" or blk == 7
            if need_v:
                if sample:
                    if br == "A":
                        d = [(O[vname[1]][l, ti, 448:512, :], 64)]
                    else:
                        d = [(O[vname[1]][l, ti], 64)]
                elif br == "A":
                    d = [(O[vname[0]][l, ti * 128:(ti + 1) * 128, :], 128)]
                else:
                    r0 = blk * BT + ti * 128
                    d = [(O[vname[0]][l, r0:r0 + 128, :], 128)]
                self.stage_out(B, bk, d)
        if _DEBUG.get("skip_attn"):
            return
        ob = blk % 2
        oT = B["oT"][ob]
        oname = "a_oT%d" % ob
        if not sample:
            if br == "A":
                self.attn_A(B, blk, None, oT, oname)
            elif br == "B":
                self.attn_B(B, blk, None, oT, oname)
            else:
                self.attn_C(B, blk, None, oT, oname)
        else:
            for s in range(4):
                if s >= 2:
                    self.load_cache(B, s)
                self.load_cache_tr(B, s)
                if _DEBUG.get("skip_sattn"):
                    continue
                if br == "A":
                    self.attn_A(B, blk, s, oT, oname)
                elif br == "B":
                    self.attn_B(B, blk, s, oT, oname)
                else:
                    self.attn_C(B, blk, s, oT, oname)
        bi = "ABC".index(br)
        self.dma("sp", self.oT_d[bi][blk], oT[:], [oname], ["oT_d%d.%d" % (bi, blk)], "a_oo%d" % ob)

    def load_cache(self, B, s):
        S = self.S
        I, O = self.I, self.O
        l, br = B["l"], B["br"]
        ntile = 4 if br == "A" else 8
        ck = I[{"A": "cak", "B": "cbk", "C": "cck"}[br]][l, s]
        cv = I[{"A": "cav", "B": "cbv", "C": "ccv"}[br]][l, s]
        reg = s % 2
        kct = B["kctok"][reg]
        self.dma("pool", kct[:, 0:ntile, :], ck.rearrange("(t p) c -> p t c", p=128), [], ["a_kct%d" % reg],
                 "a_ck%d" % reg)
        vt0 = reg * ntile
        for t in range(ntile):
            rows = cv[t * 128:(t + 1) * 128, :]
            if br == "A":
                vdst = B["V"][:, vt0 + t, :, 0:64]
                vsrc = rows.rearrange("p (h d) -> p h d", h=8)
            else:
                vdst = B["V"][:, vt0 + t, :]
                vsrc = rows
            self.dma("pool", vdst, vsrc, [], ["a_v.%d" % (vt0 + t)], "a_cv%d" % (t % 2))
        if br == "A":
            self.dma("sp", O["sak"][l, s, 0:448, :], ck[64:512, :], [], [], "a_dd0")
            self.dma("sp", O["sav"][l, s, 0:448, :], cv[64:512, :], [], [], "a_dd1")

    def load_cache_tr(self, B, s):
        S = self.S
        l, br = B["l"], B["br"]
        ntile = 4 if br == "A" else 8
        reg = s % 2
        kct = B["kctok"][reg]
        kt0 = reg * ntile * 128
        nb = 0
        for oc in range(4):
            for t0 in range(0, ntile, 4):
                bk = nb % 2
                nb += 1
                psT = self.ps[bk][:].bitcast(BF16)
                for t in range(4):
                    S.op("pe", lambda h, t=t, t0=t0, oc=oc, psT=psT: h.transpose(
                        out=psT[:, t * 128:(t + 1) * 128], in_=kct[:, t0 + t, oc * 128:(oc + 1) * 128],
                        identity=self.ident[:]), reads=["a_kct%d" % reg, "ident"], writes=["ps%d" % bk])
                c0 = kt0 + t0 * 128
                if br == "A":
                    names = ["a_kt.%d" % reg]
                else:
                    names = ["a_kt.%d" % ((c0 // 512) + i) for i in range(1)]
                self.cp(self.evac_eng(), B["KT"][:, oc, c0:c0 + 512], psT[:, 0:512], ["ps%d" % bk], names)

    def key_tiles(self, B, blk, s):
        br = B["br"]
        tiles = []
        if s is None:
            if br == "A":
                for t in range(8):
                    jb = blk - 1 if t < 4 else blk
                    if jb < 0:
                        continue
                    pos = jb % 2
                    c0 = pos * 512 + (t % 4) * 128
                    tiles.append(dict(kT=(lambda oc, c0=c0: B["KT"][:, oc, c0:c0 + 128]), kname="a_kt.%d" % pos,
                                      vi=pos * 4 + t % 4, new=False, nk=128, t=t))
            else:
                for tt in range(4 * blk + 4):
                    tiles.append(dict(kT=(lambda oc, tt=tt: B["KT"][:, oc, tt * 128:(tt + 1) * 128]),
                                      kname="a_kt.%d" % (tt // 4), vi=tt, new=False, nk=128, trel=tt - 4 * blk))
        else:
            reg = s % 2
            ntile = 4 if br == "A" else 8
            for t in range(ntile):
                c0 = reg * ntile * 128 + t * 128
                kn = "a_kt.%d" % reg if br == "A" else "a_kt.%d" % (c0 // 512)
                d = dict(kT=(lambda oc, c0=c0: B["KT"][:, oc, c0:c0 + 128]), kname=kn, vi=reg * ntile + t,
                         new=False, nk=128)
                if br == "A":
                    d["t"] = t
                else:
                    d["trel"] = t - 8
                tiles.append(d)
            d = dict(kT=(lambda oc: B["KTn"][:, oc, s * 128:s * 128 + 128]), kname="a_ktn", vi=s, new=True, nk=128)
            if br == "A":
                d["t"] = 4
            else:
                d["trel"] = 0
            tiles.append(d)
        return tiles

    def attn_A(self, B, blk, s, oT, oname):
        S = self.S
        l = B["l"]
        tiles = self.key_tiles(B, blk, s)
        Qz, PT, otok, dd, tza = B["Qz"], B["PT"], B["otok"], B["dd"], B["tza"]
        qb = 0 if s is None else s * 128
        nu = 4 if s is None else 1
        for pair in range(4):
            heads = (2 * pair, 2 * pair + 1)
            obase = 6 if pair % 2 == 0 else 0
            steps = []
            for td in tiles:
                t = td["t"]
                if s is None:
                    u0, u1 = max(0, t - 4), min(3, t)
                else:
                    u0, u1 = 0, 0
                steps.append((td, u0, u1))

            def emit_qk(si):
                td, u0, u1 = steps[si]
                nk = td["nk"]
                q0, q1 = qb + u0 * 128, qb + (u1 + 1) * 128
                c0 = (u0 - td["t"] + 4) * 128
                for ln in range(2):
                    hh = heads[ln]
                    bk = 2 + ln * 2 + si % 2
                    self.mm(self.ps[bk][0:nk, 0:q1 - q0], td["kT"](pair), Qz[hh][:, q0:q1], True, False,
                            [td["kname"], "a_qz%d" % hh], ["ps%d" % bk])
                    self.mm(self.ps[bk][0:nk, 0:q1 - q0], self.ident[0:nk, 0:nk], tza[0:nk, hh, c0:c0 + (q1 - q0)],
                            False, True, ["ident", "a_tza"], ["ps%d" % bk])

            first = [True, True]

            def emit_pv(si):
                td, u0, u1 = steps[si]
                nk = td["nk"]
                for ln in range(2):
                    hh = heads[ln]
                    pt = ln * 2 + si % 2
                    ob = obase + ln
                    if td["new"]:
                        vap, vn = B["Vn"][0:nk, td["vi"], hh, :], "a_vn.%d" % td["vi"]
                    else:
                        vap, vn = B["V"][0:nk, td["vi"], hh, :], "a_v.%d" % td["vi"]
                    for u in range(u0, u1 + 1):
                        self.mm(self.ps[ob][:, u * 65:(u + 1) * 65], PT[pt][0:nk, (u - u0) * 128:(u - u0 + 1) * 128],
                                vap, first[ln], False, ["a_pt%d" % pt, vn], ["ps%d" % ob])
                        first[ln] = False

            emit_qk(0)
            for si in range(len(steps)):
                if si + 1 < len(steps):
                    emit_qk(si + 1)
                td, u0, u1 = steps[si]
                nk = td["nk"]
                n = (u1 - u0 + 1) * 128
                for ln in range(2):
                    bk = 2 + ln * 2 + si % 2
                    pt = ln * 2 + si % 2
                    self.act(PT[pt][0:nk, 0:n], self.ps[bk][0:nk, 0:n], AF.Exp, ["ps%d" % bk], ["a_pt%d" % pt])
                if si > 0:
                    emit_pv(si - 1)
            emit_pv(len(steps) - 1)
            for ln in range(2):
                hh = heads[ln]
                ob = obase + ln
                den = self.ps[ob][:, 0:nu * 65].rearrange("p (u c) -> p u c", c=65)[:, :, 64]
                self.ts("dve", dd[:, ln, 0:nu], den, 1e-30, None, ALU.max, None, ["ps%d" % ob], ["a_dd"])
                S.op("dve", lambda h, ln=ln: h.reciprocal(out=dd[:, ln, 4:4 + nu], in_=dd[:, ln, 0:nu]),
                     reads=["a_dd"], writes=["a_dd"])
                for u in range(nu):
                    uu = u if s is None else s
                    self.ts("dve", otok[:, uu, hh * 64:(hh + 1) * 64], self.ps[ob][:, u * 65:u * 65 + 64],
                            dd[:, ln, 4 + u:5 + u], None, ALU.mult, None, ["ps%d" % ob, "a_dd"], ["a_otok"])
        self.otok_to_oT(B, s, oT, oname)

    def otok_to_oT(self, B, s, oT, oname):
        S = self.S
        otok = B["otok"]
        us = range(4) if s is None else [s]
        for kc in range(4):
            bk = kc % 2
            psT = self.ps[bk][:].bitcast(BF16)
            for u in us:
                S.op("pe", lambda h, u=u, kc=kc, psT=psT: h.transpose(
                    out=psT[:, u * 128:(u + 1) * 128], in_=otok[:, u, kc * 128:(kc + 1) * 128],
                    identity=self.ident[:]), reads=["a_otok", "ident"], writes=["ps%d" % bk])
            if s is None:
                self.cp(self.evac_eng(), oT[:, kc, :], psT[:, 0:512], ["ps%d" % bk], [oname])
            else:
                self.cp(self.evac_eng(), oT[:, kc, s * 128:(s + 1) * 128], psT[:, s * 128:(s + 1) * 128],
                        ["ps%d" % bk], [oname])

    def attn_B(self, B, blk, s, oT, oname):
        S = self.S
        tiles = list(reversed(self.key_tiles(B, blk, s)))
        Qz, LP, LS, WT = B["Qz"], B["LP"], B["LS"], B["WT"]
        G = B["G"]
        qb = 0 if s is None else s * 128
        nq = 512 if s is None else 128
        nt = len(tiles)

        def rng(td):
            trel = td["trel"]
            q0 = max(0, trel) * 128
            return q0, nq - q0, trel >= 0

        for pair in range(4):
            heads = (2 * pair, 2 * pair + 1)
            obase = 6 if pair % 2 == 0 else 0

            def qk(si, stop):
                td = tiles[si]
                q0, n, diag = rng(td)
                for ln in range(2):
                    hh = heads[ln]
                    bk = 2 + ln * 2 + si % 2
                    self.mm(self.ps[bk][:, q0:q0 + n], td["kT"](pair), Qz[hh][:, qb + q0:qb + q0 + n], True, stop,
                            [td["kname"], "a_qz%d" % hh], ["ps%d" % bk])

            def pe_group(si):
                td = tiles[si]
                q0, n, diag = rng(td)
                qk(si, False)
                for ln in range(2):
                    bk = 2 + ln * 2 + si % 2
                    lp = ln * G + si % G
                    self.mm(self.ps[bk][:, q0:q0 + n], self.negtri[:], LP[lp][:, q0:q0 + n], False, False,
                            ["negtri", "b_lp%d" % lp], ["ps%d" % bk])
                    if si > 0:
                        self.mm(self.ps[bk][:, q0:q0 + n], self.negones[:], LS[ln][:, q0:q0 + n], False, False,
                                ["negones", "b_ls%d" % ln], ["ps%d" % bk])
                    if diag:
                        self.mm(self.ps[bk][:, q0:q0 + n], self.ident[:], self.mkneg[:, 0:n], False, True,
                                ["ident", "mkneg"], ["ps%d" % bk])

            for ln in range(2):
                self.memset("dve", LS[ln][:], 0.0, ["b_ls%d" % ln])
            for g0 in range(0, nt, G):
                g1 = min(nt, g0 + G)
                qk(g0, True)
                for si in range(g0, g1):
                    if si + 1 < g1:
                        qk(si + 1, True)
                    td = tiles[si]
                    q0, n, diag = rng(td)
                    for ln in range(2):
                        bk = 2 + ln * 2 + si % 2
                        lp = ln * G + si % G
                        self.act(LP[lp][:, q0:q0 + n], self.ps[bk][:, q0:q0 + n], AF.Softplus, ["ps%d" % bk],
                                 ["b_lp%d" % lp])
                        if diag:
                            self.tt("dve", LP[lp][:, q0:q0 + n], LP[lp][:, q0:q0 + n], self.mk01[:, 0:n],
                                    ALU.mult, ["b_lp%d" % lp, "mk01"], ["b_lp%d" % lp])
                pe_group(g0)
                for si in range(g0, g1):
                    td = tiles[si]
                    q0, n, diag = rng(td)
                    last = (si == nt - 1)
                    if not last:
                        for ln in range(2):
                            lp = ln * G + si % G
                            self.tt("dve", LS[ln][:, q0:q0 + n], LS[ln][:, q0:q0 + n], LP[lp][:, q0:q0 + n], ALU.add,
                                    ["b_ls%d" % ln, "b_lp%d" % lp], ["b_ls%d" % ln])
                    if si + 1 < g1:
                        pe_group(si + 1)
                    for ln in range(2):
                        bk = 2 + ln * 2 + si % 2
                        wt = ln * 2 + si % 2
                        self.act(WT[wt][:, q0:q0 + n], self.ps[bk][:, q0:q0 + n], AF.Exp, ["ps%d" % bk],
                                 ["b_wt%d" % wt])
                    for ln in range(2):
                        wt = ln * 2 + si % 2
                        ob = obase + ln
                        if td["new"]:
                            vap, vn = B["Vn"][:, td["vi"], pair * 128:(pair + 1) * 128], "a_vn.%d" % td["vi"]
                        else:
                            vap, vn = B["V"][:, td["vi"], pair * 128:(pair + 1) * 128], "a_v.%d" % td["vi"]
                        self.mm(self.ps[ob][:, q0:q0 + n], vap, WT[wt][:, q0:q0 + n], si == 0, last,
                                ["b_wt%d" % wt, vn], ["ps%d" % ob])
            self.act(oT[0:64, pair, qb:qb + nq], self.ps[obase][0:64, 0:nq], AF.Copy, ["ps%d" % obase], [oname])
            self.cp("dve", oT[64:128, pair, qb:qb + nq], self.ps[obase + 1][64:128, 0:nq], ["ps%d" % (obase + 1)],
                    [oname])

    def attn_C(self, B, blk, s, oT, oname):
        S = self.S
        l = B["l"]
        tiles = self.key_tiles(B, blk, s)
        Qz, PT, PA, cw = B["Qz"], B["PT"], B["PA"], B["cw"]
        qb = 0 if s is None else s * 128
        nq = 512 if s is None else 128
        pending = [None]
        for hd in range(4):
            def rng(td):
                trel = td["trel"]
                u0 = max(0, trel)
                return u0, nq - u0 * 128, trel >= -1

            def emit_qk(si):
                td = tiles[si]
                u0, n, near = rng(td)
                q0 = u0 * 128
                c0 = (u0 - td["trel"]) * 128
                for m in range(2):
                    bk = m * 2 + si % 2
                    self.mm(self.ps[bk][:, q0:q0 + n], td["kT"](hd), Qz[2 * hd + m][:, qb + q0:qb + q0 + n], True,
                            not near, [td["kname"], "a_qz%d" % (2 * hd + m)], ["ps%d" % bk])
                    if near:
                        self.mm(self.ps[bk][:, q0:q0 + n], self.ident[:], self.tzc[:, hd, c0:c0 + n],
                                False, True, ["ident", "tzc"], ["ps%d" % bk])

            def emit_pv(si):
                td = tiles[si]
                u0, n, near = rng(td)
                q0 = u0 * 128
                last = si == len(tiles) - 1
                for m in range(2):
                    pt = m * 2 + si % 2
                    if td["new"]:
                        vap, vn = B["Vn"][:, td["vi"], hd * 128:(hd + 1) * 128], "a_vn.%d" % td["vi"]
                    else:
                        vap, vn = B["V"][:, td["vi"], hd * 128:(hd + 1) * 128], "a_v.%d" % td["vi"]
                    self.mm(self.ps[4 + m][:, q0:q0 + n], vap, PT[pt][:, q0:q0 + n], si == 0, last,
                            ["a_pt%d" % pt, vn], ["ps%d" % (4 + m)])
                    if m == 0:
                        self.tt("dve", PA[m][:, q0:q0 + n], PA[m][:, q0:q0 + n], PT[pt][:, q0:q0 + n], ALU.add,
                                ["c_pa%d" % m, "a_pt%d" % pt], ["c_pa%d" % m])
                    else:
                        self.mm(self.ps[7][:, q0:q0 + n], self.posones[:], PT[pt][:, q0:q0 + n], si == 0, last,
                                ["a_pt%d" % pt, "posones"], ["ps7"])

            self.memset("dve", PA[0][:, 0:nq], 0.0, ["c_pa0"])
            emit_qk(0)
            for si in range(len(tiles)):
                if si + 1 < len(tiles):
                    emit_qk(si + 1)
                td = tiles[si]
                u0, n, near = rng(td)
                q0 = u0 * 128
                last = si == len(tiles) - 1
                for m in range(2):
                    bk = m * 2 + si % 2
                    pt = m * 2 + si % 2
                    if near:
                        self.act(PT[pt][:, q0:q0 + n], self.ps[bk][:, q0:q0 + n], AF.Exp, ["ps%d" % bk],
                                 ["a_pt%d" % pt])
                    else:
                        self.act(PT[pt][:, q0:q0 + n], self.ps[bk][:, q0:q0 + n], AF.Exp, ["ps%d" % bk, "c15"],
                                 ["a_pt%d" % pt], bias=self.c15[:, hd:hd + 1])
                if si > 0:
                    emit_pv(si - 1)
                if si == 2 and pending[0] is not None:
                    pending[0]()
                    pending[0] = None
            emit_pv(len(tiles) - 1)
            PAb, R, O32, SQ, OS = cw["pab"], cw["r"], cw["o32"], cw["sq"], cw["os"]
            self.cp("act", OS[0][:, 0:nq], self.ps[4][:, 0:nq], ["ps4"], ["c_os0"])
            self.cp("dve", OS[1][:, 0:nq], self.ps[5][:, 0:nq], ["ps5"], ["c_os1"])
            self.act(R[1][:, 0:nq], self.ps[7][:, 0:nq], AF.Ln, ["ps7"], ["c_r1"])
            self.cp("act", PAb[0][:, 0:nq], PA[0][:, 0:nq], ["c_pa0"], ["c_pab0"])
            self.mm(self.ps[6][:, 0:nq], self.posones[:], PAb[0][:, 0:nq], True, True, ["posones", "c_pab0"], ["ps6"])

            def part2(hd=hd):
                self.act(R[1][:, 0:nq], R[1][:, 0:nq], AF.Exp, ["c_r1"], ["c_r1"], scale=-1.0)
                self.act(R[0][:, 0:nq], self.ps[6][:, 0:nq], AF.Ln, ["ps6"], ["c_r0"])
                self.act(R[0][:, 0:nq], R[0][:, 0:nq], AF.Exp, ["c_r0"], ["c_r0"], scale=-1.0)
                self.tt("dve", R[1][:, 0:nq], R[1][:, 0:nq], OS[1][:, 0:nq], ALU.mult, ["c_r1", "c_os1"], ["c_r1"])
                self.tt("dve", R[0][:, 0:nq], R[0][:, 0:nq], OS[0][:, 0:nq], ALU.mult, ["c_r0", "c_os0"], ["c_r0"])
                self.stt("dve", O32[:, 0:nq], R[1][:, 0:nq], self.neglam[:, l:l + 1], R[0][:, 0:nq], ALU.mult,
                         ALU.add, ["c_r0", "c_r1", "neglam"], ["c_o32"])
                self.act(SQ[:, 0:nq], O32[:, 0:nq], AF.Square, ["c_o32"], ["c_sq"])
                self.mm(self.ps[6][:, 0:nq], self.posones[:], SQ[:, 0:nq], True, True, ["posones", "c_sq"], ["ps6"])
                self.ts("dve", R[0][:, 0:nq], self.ps[6][:, 0:nq], 1.0 / 128, EPS, ALU.mult, ALU.add, ["ps6"],
                        ["c_r0"])
                self.act(R[0][:, 0:nq], R[0][:, 0:nq], AF.Ln, ["c_r0"], ["c_r0"])
                self.act(R[0][:, 0:nq], R[0][:, 0:nq], AF.Exp, ["c_r0"], ["c_r0"], scale=-0.5)
                self.stt("dve", oT[:, hd, qb:qb + nq], O32[:, 0:nq], self.sublnT[:, l:l + 1], R[0][:, 0:nq], ALU.mult,
                         ALU.mult, ["c_o32", "sublnT", "c_r0"], [oname])

            pending[0] = part2
        if pending[0] is not None:
            pending[0]()
            pending[0] = None

    def phase_M(self, l):
        S = self.S
        I = self.I
        with ExitStack() as ph:
            wG = self.sb(ph, "m_wg", [128, 8, 3072], BF16)
            wB = self.sb(ph, "m_wb", [128, 12, D], BF16)
            wO = self.sb(ph, "m_wo", [128, 8, D], BF16)
            self.load_w(wG, I["w_in"][l][:, 4608:7680], "m_wg", 4, "m_wg")
            self.load_w(wB, I["w_branch"][l].rearrange("n r c -> (n r) c"), "m_wb", 4, "m_wb")
            self.load_w(wO, I["w_out"][l], "m_wo", 4, "m_wo")
            xnT = [self.sb(ph, "m_xnT%d" % i, [128, 8, BT], BF16) for i in range(2)]
            oT = [[self.sb(ph, "m_oT%d_%d" % (n, i), [128, 4, BT], BF16) for i in range(2)] for n in range(3)]
            gs = [self.sb(ph, "m_g%d" % i, [128, BT], F32) for i in range(3)]
            tm = [self.sb(ph, "m_t%d" % i, [128, BT], F32) for i in range(3)]
            hacc = self.sb(ph, "m_hacc", [128, BT], F32)
            hT = self.sb(ph, "m_hT", [128, 8, BT], BF16)
            xt = [self.sb(ph, "m_xt%d" % i, [128, D], F32) for i in range(2)]

            def loads(blk):
                s_ = blk % 2
                self.dma("sp", xnT[s_][:], self.xnT_d[blk], ["xnT_d.%d" % blk], ["m_xnT%d" % s_], "m_x%d" % s_)
                for n in range(3):
                    self.dma("sp", oT[n][s_][:], self.oT_d[n][blk], ["oT_d%d.%d" % (n, blk)],
                             ["m_oT%d_%d" % (n, s_)], "m_o%d_%d" % (n, s_))

            loads(0)
            xk = 0
            for blk in range(NBLK):
                s_ = blk % 2
                if blk + 1 < NBLK:
                    loads(blk + 1)
                for dc in range(8):
                    for n in range(3):
                        gb = n
                        bb = 3 + n
                        for kc in range(8):
                            self.mm(self.ps[gb][:], wG[:, kc, n * D + dc * 128:n * D + (dc + 1) * 128],
                                    xnT[s_][:, kc, :], kc == 0, kc == 7, ["m_wg.%d" % kc, "m_xnT%d" % s_],
                                    ["ps%d" % gb])
                        self.act(gs[n][:], self.ps[gb][:], AF.Sigmoid, ["ps%d" % gb, "bgate"], ["m_g%d" % n],
                                 bias=self.bgate[:, l, n, dc:dc + 1])
                        for kc in range(4):
                            self.mm(self.ps[bb][:], wB[:, n * 4 + kc, dc * 128:(dc + 1) * 128], oT[n][s_][:, kc, :],
                                    kc == 0, kc == 3, ["m_wb.%d" % (n * 4 + kc), "m_oT%d_%d" % (n, s_)],
                                    ["ps%d" % bb])
                        self.tt("dve", tm[n][:], gs[n][:], self.ps[bb][:], ALU.mult, ["m_g%d" % n, "ps%d" % bb],
                                ["m_t%d" % n])
                    self.tt("dve", hacc[:], tm[0][:], tm[1][:], ALU.add, ["m_t0", "m_t1"], ["m_hacc"])
                    self.tt("dve", hT[:, dc, :], hacc[:], tm[2][:], ALU.add, ["m_hacc", "m_t2"], ["m_hT"])
                for ti in range(4):
                    xs_ = xk % 2
                    xk += 1
                    self.dma("sp", xt[xs_][:], self.x_tile_src(l == 0, blk, ti), ["xres.%d.%d" % (blk, ti)],
                             ["m_xt%d" % xs_], "m_xl%d" % xs_)
                    for half in range(2):
                        ob = 6 + half
                        for kc in range(8):
                            self.mm(self.ps[ob][:], hT[:, kc, ti * 128:(ti + 1) * 128],
                                    wO[:, kc, half * 512:(half + 1) * 512], kc == 0, kc == 7,
                                    ["m_hT", "m_wo.%d" % kc], ["ps%d" % ob])
                        self.tt("dve", xt[xs_][:, half * 512:(half + 1) * 512], xt[xs_][:, half * 512:(half + 1) * 512],
                                self.ps[ob][:], ALU.add, ["m_xt%d" % xs_, "ps%d" % ob], ["m_xt%d" % xs_])
                    r0 = blk * BT + ti * 128
                    self.dma("sp", self.xres[r0:r0 + 128, :], xt[xs_][:], ["m_xt%d" % xs_],
                             ["xres.%d.%d" % (blk, ti)], "m_xs%d" % xs_)
            S.barrier()
            S.replay()

    def phase_F(self, l, wts=None):
        S = self.S
        I, O = self.I, self.O
        with ExitStack() as ph:
            if wts is None:
                wts = self.pre_F(ph, l)
            wU, wD = wts
            xnT = [self.sb(ph, "f_xnT%d" % i, [128, 8, BT], BF16) for i in range(2)]
            gsb = [self.sb(ph, "f_g%d" % i, [128, BT + 2], F32) for i in range(2)]
            cc = [self.sb(ph, "f_c%d" % i, [128, BT], F32) for i in range(2)]
            sg = [self.sb(ph, "f_s%d" % i, [128, BT], F32) for i in range(2)]
            usb = [self.sb(ph, "f_u%d" % i, [128, BT], F32) for i in range(2)]
            halo = self.sb(ph, "f_halo", [128, NFC, 2], F32)
            cvo = self.sb(ph, "f_cvo", [128, NFC, 4, 2], F32)
            hf = self.sb(ph, "f_hf", [128, NFC, BT], BF16)
            xt = [self.sb(ph, "f_xt%d" % i, [128, D], F32) for i in range(2)]
            self.memset("pool", halo[:], 0.0, ["f_halo"])
            self.dma("sp", xnT[0][:], self.xnT_d[0], ["xnT_d.0"], ["f_xnT0"], "f_x0")
            xk = 0
            for blk in range(NBLK):
                s_ = blk % 2
                sample = blk == 8
                if blk + 1 < NBLK:
                    self.dma("sp", xnT[1 - s_][:], self.xnT_d[blk + 1], ["xnT_d.%d" % (blk + 1)],
                             ["f_xnT%d" % (1 - s_)], "f_x%d" % (1 - s_))
                def s0(fc):
                    k2 = fc % 2
                    gbk, ubk = k2, 2 + k2
                    for kc in range(8):
                        self.mm(self.ps[gbk][:], wU[:, kc, fc * 128:(fc + 1) * 128], xnT[s_][:, kc, :], kc == 0,
                                kc == 7, ["f_wu.%d" % kc, "f_xnT%d" % s_], ["ps%d" % gbk])
                    for kc in range(8):
                        self.mm(self.ps[ubk][:], wU[:, kc, DFF + fc * 128:DFF + (fc + 1) * 128], xnT[s_][:, kc, :],
                                kc == 0, kc == 7, ["f_wu.%d" % kc, "f_xnT%d" % s_], ["ps%d" % ubk])

                def s1(fc):
                    k2 = fc % 2
                    gbk, ubk = k2, 2 + k2
                    gname = "f_g%d" % k2
                    G = gsb[k2]
                    self.act(G[:, 2:BT + 2], self.ps[gbk][:], AF.Copy, ["ps%d" % gbk], [gname])
                    self.act(usb[k2][:], self.ps[ubk][:], AF.Copy, ["ps%d" % ubk], ["f_u%d" % k2])
                    if sample:
                        self.cp("pool", G[:, 0:BT].rearrange("p (s c) -> p s c", c=128)[:, :, 0:2],
                                self.cst[:, l, fc, :, :], ["cst", gname], [gname])
                    else:
                        self.cp("pool", G[:, 0:2], halo[:, fc, :], ["f_halo", gname], [gname])
                        if blk < 7:
                            self.cp("pool", halo[:, fc, :], G[:, BT:BT + 2], [gname], ["f_halo"])
                        elif blk == 7:
                            self.cp("pool", cvo[:, fc, 0, :], G[:, BT:BT + 2], [gname], ["f_cvo"])
                    if sample:
                        self.cp("pool", cvo[:, fc, :, :],
                                G[:, 2:BT + 2].rearrange("p (s c) -> p s c", c=128)[:, :, 62:64], [gname], ["f_cvo"])

                def s2(fc):
                    k2 = fc % 2
                    gname, cn = "f_g%d" % k2, "f_c%d" % k2
                    G, C_ = gsb[k2], cc[k2]
                    self.ts("dve", C_[:], G[:, 2:BT + 2], self.convw[:, l, 2, fc:fc + 1],
                            self.convb[:, l, fc:fc + 1], ALU.mult, ALU.add, [gname, "convw", "convb"], [cn])
                    self.stt("dve", C_[:], G[:, 1:BT + 1], self.convw[:, l, 1, fc:fc + 1], C_[:], ALU.mult, ALU.add,
                             [gname, "convw", cn], [cn])
                    self.stt("dve", C_[:], G[:, 0:BT], self.convw[:, l, 0, fc:fc + 1], C_[:], ALU.mult, ALU.add,
                             [gname, "convw", cn], [cn])

                def s3(fc):
                    k2 = fc % 2
                    self.act(sg[k2][:], cc[k2][:], AF.Gelu_apprx_tanh, ["f_c%d" % k2], ["f_s%d" % k2])

                def s4(fc):
                    k2 = fc % 2
                    self.tt("dve", hf[:, fc, :], sg[k2][:], usb[k2][:], ALU.mult, ["f_s%d" % k2, "f_u%d" % k2],
                            ["f_hf"])

                for k in range(NFC + 2):
                    if k < NFC:
                        s0(k)
                    if 0 <= k - 2 < NFC:
                        s3(k - 2)
                    if 0 <= k - 1 < NFC:
                        s1(k - 1)
                    if 0 <= k - 2 < NFC:
                        s4(k - 2)
                    if 0 <= k - 1 < NFC:
                        s2(k - 1)
                if blk == 7:
                    for t_ in range(2):
                        self.dma("pool", O["pconv"][l, t_].rearrange("(c p) -> p c", p=128), cvo[:, :, 0, t_],
                                 ["f_cvo"], [], "f_co%d" % t_, slow=True)
                if sample:
                    for s in range(4):
                        for t_ in range(2):
                            self.dma("pool", O["sconv"][l, s, t_].rearrange("(c p) -> p c", p=128), cvo[:, :, s, t_],
                                     ["f_cvo"], [], "f_co%d" % (s * 2 + t_), slow=True)
                for ti in range(4):
                    xs_ = xk % 2
                    xk += 1
                    r0 = blk * BT + ti * 128
                    self.dma("sp", xt[xs_][:], self.xres[r0:r0 + 128, :], ["xres.%d.%d" % (blk, ti)],
                             ["f_xt%d" % xs_], "f_xl%d" % xs_)
                    for half in range(2):
                        ob = 4 + half
                        for fc in range(NFC):
                            self.mm(self.ps[ob][:], hf[:, fc, ti * 128:(ti + 1) * 128],
                                    wD[:, fc, half * 512:(half + 1) * 512], fc == 0, fc == NFC - 1,
                                    ["f_hf", "f_wd.%d" % fc], ["ps%d" % ob])
                        self.tt("dve", xt[xs_][:, half * 512:(half + 1) * 512], xt[xs_][:, half * 512:(half + 1) * 512],
                                self.ps[ob][:], ALU.add, ["f_xt%d" % xs_, "ps%d" % ob], ["f_xt%d" % xs_])
                    self.dma("sp", self.xres[r0:r0 + 128, :], xt[xs_][:], ["f_xt%d" % xs_],
                             ["xres.%d.%d" % (blk, ti)], "f_xs%d" % xs_)
            S.barrier()
            S.replay()


def _t5_bucket(rel):
    half, max_exact, max_dist = 16, 8, 128
    ret = np.where(rel > 0, half, 0)
    n = np.abs(rel)
    nf = np.maximum(n, 1).astype(np.float32)
    large = max_exact + (np.log(nf / np.float32(max_exact)) / np.float32(np.log(max_dist / max_exact))
                         * np.float32(half - max_exact)).astype(np.int32)
    large = np.minimum(large, half - 1)
    return ret + np.where(n < max_exact, n, large)


def _consts(a_rel_bias, t5_bias):
    k = np.arange(128)[:, None]
    m = np.arange(640)[None, :]
    b_, qq = m // 128, m % 128
    rel = b_ * 128 + qq - k
    idx = np.clip(rel, -128, 128) + 128
    chunk = 2 * b_ + qq // 64 - k // 64
    valid = (chunk >= 0) & (chunk <= 8)
    tza = np.empty((2, 128, 8, 640), np.float32)
    for l in range(2):
        for h in range(8):
            tza[l, :, h, :] = np.where(valid, a_rel_bias[l][idx, h], np.float32(NEG))
    relc = k - qq - b_ * 128
    bucket = _t5_bucket(relc)
    maskc = (k // 64 - qq // 64) > 2 * b_
    tzc = np.empty((128, 4, 640), np.float32)
    for h in range(4):
        tzc[:, h, :] = np.where(maskc, np.float32(NEG), t5_bias[bucket, h])
    c15 = np.broadcast_to(t5_bias[15][None, :], (128, 4)).astype(np.float32).copy()
    q = np.arange(512)[None, :]
    mk01 = (k < q).astype(np.float32)
    mkneg = np.where(k < q, np.float32(0.0), np.float32(NEG)).astype(np.float32)
    return tza, tzc, c15, mk01, mkneg


_NC_CACHE = {}
_DEBUG = {}
PCORES = [0, 1, 4, 5]


def kernel(x_prompt, x_sample, cache_a_k, cache_a_v, cache_b_k, cache_b_v, cache_c_k, cache_c_v,
           state_ffn_conv, norm_mix, w_in, b_gate, a_rel_bias, t5_bias, c_lambda, c_subln,
           w_branch, w_out, norm_ffn, w_up, conv_w, conv_b, w_down, norm_final):
    f = lambda a: np.ascontiguousarray(np.asarray(a, dtype=np.float32))
    x_prompt, x_sample = f(x_prompt), f(x_sample)
    w_in, w_branch, w_out, w_up, w_down = f(w_in), f(w_branch), f(w_out), f(w_up), f(w_down)
    a_rel_bias, t5_bias = f(a_rel_bias), f(t5_bias)
    tza, tzc, c15, mk01, mkneg = _consts(a_rel_bias, t5_bias)
    gains = np.stack([f(norm_mix)[0], f(norm_ffn)[0], f(norm_mix)[1], f(norm_ffn)[1], f(norm_final)], 0)
    gains = np.ascontiguousarray(np.broadcast_to(gains[:, None, :], (5, 128, D)))
    bgate = np.ascontiguousarray(f(b_gate).reshape(2, 3, 8, 128).transpose(3, 0, 1, 2))
    convw = np.ascontiguousarray(f(conv_w).reshape(2, 3, NFC, 128).transpose(3, 0, 1, 2))
    convb = np.ascontiguousarray(f(conv_b).reshape(2, NFC, 128).transpose(2, 0, 1))
    lam = np.ascontiguousarray(np.broadcast_to(f(c_lambda).reshape(1, 2, 256), (128, 2, 256)))
    subln = np.ascontiguousarray(np.broadcast_to(f(c_subln).reshape(1, 2, 128), (128, 2, 128)))
    sublnT = np.ascontiguousarray(f(c_subln).T)
    cak, cav = f(cache_a_k).reshape(2, 32, 512, 512), f(cache_a_v).reshape(2, 32, 512, 512)
    cbk, cbv = f(cache_b_k).reshape(2, 32, 1024, 512), f(cache_b_v).reshape(2, 32, 1024, 512)
    cck, ccv = f(cache_c_k).reshape(2, 32, 1024, 512), f(cache_c_v).reshape(2, 32, 1024, 512)
    cst_all = f(state_ffn_conv)
    shared = dict(w_in=w_in, w_branch=w_branch, w_out=w_out, w_up=w_up, w_down=w_down, gains=gains, bgate=bgate,
                  convw=convw, convb=convb, lam=lam, subln=subln, sublnT=sublnT, tza=tza, tzc=tzc, c15=c15, mk01=mk01, mkneg=mkneg)
    in_maps = []
    zero_xp = np.zeros((SEQ, D), np.float32)
    for c in range(8):
        sl = slice(4 * c, 4 * c + 4)
        xs = np.zeros((4, 128, D), np.float32)
        xs[:, 0:64, :] = x_sample[sl]
        cst = cst_all[:, sl].reshape(2, 4, 2, NFC, 128).transpose(4, 0, 3, 1, 2)
        m = dict(shared)
        xp_c = x_prompt[PCORES.index(c)] if c in PCORES else zero_xp
        m.update(xp=xp_c, xs=xs.reshape(BT, D),
                 cak=np.ascontiguousarray(cak[:, sl]), cav=np.ascontiguousarray(cav[:, sl]),
                 cbk=np.ascontiguousarray(cbk[:, sl]), cbv=np.ascontiguousarray(cbv[:, sl]),
                 cck=np.ascontiguousarray(cck[:, sl]), ccv=np.ascontiguousarray(ccv[:, sl]),
                 cst=np.ascontiguousarray(cst))
        in_maps.append(m)
    if _DEBUG.get("prep_only"):
        return in_maps
    if "nc" not in _NC_CACHE:
        _NC_CACHE["nc"] = Builder().build()
    res = run_bass_kernel_spmd(_NC_CACHE["nc"], in_maps, core_ids=list(range(8)))
    R = res.results
    y_prompt = np.stack([R[b]["yp"] for b in PCORES], 0)
    y_sample = np.concatenate([R[c]["ys"] for c in range(8)], 0)

    def pst(name, shape):
        return np.stack([R[b][name] for b in PCORES], 1).reshape(shape)

    def sst(name, shape):
        return np.concatenate([R[c][name] for c in range(8)], 1).reshape(shape)

    outs = (
        y_prompt, y_sample,
        pst("pak", (2, 4, 512, 8, 64)), pst("pav", (2, 4, 512, 8, 64)),
        pst("pbk", (2, 4, SEQ, 8, 64)), pst("pbv", (2, 4, SEQ, 8, 64)),
        pst("pck", (2, 4, SEQ, 4, 2, 64)), pst("pcv", (2, 4, SEQ, 4, 128)),
        pst("pconv", (2, 4, 2, DFF)),
        sst("sak", (2, 32, 512, 8, 64)), sst("sav", (2, 32, 512, 8, 64)),
        sst("sbk", (2, 32, 64, 8, 64)), sst("sbv", (2, 32, 64, 8, 64)),
        sst("sck", (2, 32, 64, 4, 2, 64)), sst("scv", (2, 32, 64, 4, 128)),
        sst("sconv", (2, 32, 2, DFF)),
    )
    return tuple(np.ascontiguousarray(o, dtype=np.float32) for o in outs)
```
